# Optimizing a Trainium2 kernel written in Bass

```python
import math
import jax, jax.numpy as jnp
from jax import lax
import numpy as np

D_MODEL = 1024
BATCH = 8
SEQ = 4096
DEPTH = 4

GRID_W = 64
NA_HEADS = 8
NA_HEAD_DIM = 64
NA_WIN_ROWS = 8
NA_WIN_COLS = 16
MLA_HEADS = 8
MLA_Q_RANK = 256
MLA_KV_RANK = 128
MLA_NOPE_DIM = 64
MLA_ROPE_DIM = 32
MLA_V_DIM = 64
ROPE_THETA = 10000.0
D_FF = 4 * D_MODEL
Q_BLOCK = 128
NORM_EPS = 1e-6
NA_WIDTH = NA_HEADS * NA_HEAD_DIM
MLA_WIDTH = MLA_HEADS * MLA_V_DIM
IN_SIZES = (NA_WIDTH, NA_WIDTH, NA_WIDTH, MLA_Q_RANK, MLA_KV_RANK, MLA_ROPE_DIM, 2 * D_MODEL)
IN_COLS = sum(IN_SIZES)
RPB_ROWS = 2 * NA_WIN_ROWS - 1
RPB_COLS = 2 * NA_WIN_COLS - 1

kernel_name = "hybrid_natten_mla_sqrelu_adaln_encoder"


def rmsnorm(x, g):
    xf = x.astype(jnp.float32)
    y = xf * lax.rsqrt(jnp.mean(xf * xf, axis=-1, keepdims=True) + NORM_EPS)
    return (y * g.astype(jnp.float32)).astype(x.dtype)


def apply_rope(x, cos, sin):
    half = x.shape[-1] // 2
    x1, x2 = x[..., :half], x[..., half:]
    return jnp.concatenate([x1 * cos - x2 * sin, x2 * cos + x1 * sin], axis=-1)


def rope_tables(positions, dtype):
    inv_freq = 1.0 / (ROPE_THETA ** (jnp.arange(0, MLA_ROPE_DIM, 2, dtype=jnp.float32) / MLA_ROPE_DIM))
    ang = positions.astype(jnp.float32)[..., None] * inv_freq
    return jnp.cos(ang).astype(dtype), jnp.sin(ang).astype(dtype)


def split_cols(t, sizes):
    idx = []
    acc = 0
    for s in sizes[:-1]:
        acc += s
        idx.append(acc)
    return jnp.split(t, idx, axis=-1)


def neighborhood_attention(q, k, v, rpb):
    B, S, H, dh = q.shape
    rows = S // GRID_W
    kh = min(NA_WIN_ROWS, rows)
    kw = NA_WIN_COLS

    def to_grid(t):
        return t.reshape(B, rows, GRID_W, H, dh).transpose(1, 0, 3, 2, 4)

    qg, kg, vg = to_grid(q), to_grid(k), to_grid(v)
    col = jnp.arange(GRID_W)
    col_start = jnp.clip(col - kw // 2, 0, GRID_W - kw)
    col_idx = col_start[:, None] + jnp.arange(kw)
    col_off = col_idx - col[:, None] + (NA_WIN_COLS - 1)
    row = jnp.arange(rows)
    row_start = jnp.clip(row - kh // 2, 0, rows - kh)
    scale = dh ** -0.5

    def one_row(args):
        r, rs, q_r = args
        k_blk = lax.dynamic_slice_in_dim(kg, rs, kh, axis=0)
        v_blk = lax.dynamic_slice_in_dim(vg, rs, kh, axis=0)
        k_nb = k_blk[:, :, :, col_idx]
        v_nb = v_blk[:, :, :, col_idx]
        s = jnp.einsum('bhcd,ibhcjd->bhcij', q_r, k_nb).astype(jnp.float32) * scale
        row_off = rs + jnp.arange(kh) - r + (NA_WIN_ROWS - 1)
        bias = rpb[:, row_off[:, None, None], col_off[None, :, :]]
        s = s + bias.transpose(0, 2, 1, 3)[None].astype(jnp.float32)
        p = jax.nn.softmax(s.reshape(B, H, GRID_W, kh * kw), axis=-1)
        p = p.reshape(B, H, GRID_W, kh, kw).astype(v.dtype)
        return jnp.einsum('bhcij,ibhcjd->bhcd', p, v_nb)

    out = lax.map(one_row, (row, row_start, qg))
    return out.transpose(1, 0, 3, 2, 4).reshape(B, S, H * dh)


def mla_attention(c_q, c_kv, k_rope, w_uq, w_ukv, g_q, g_kv, cos, sin):
    B, S, _ = c_q.shape
    H = MLA_HEADS
    q = (rmsnorm(c_q, g_q) @ w_uq).reshape(B, S, H, MLA_NOPE_DIM + MLA_ROPE_DIM)
    q_nope, q_rope = q[..., :MLA_NOPE_DIM], q[..., MLA_NOPE_DIM:]
    q_rope = apply_rope(q_rope, cos[:, :, None], sin[:, :, None])
    kv = (rmsnorm(c_kv, g_kv) @ w_ukv).reshape(B, S, H, MLA_NOPE_DIM + MLA_V_DIM)
    k_nope, v = kv[..., :MLA_NOPE_DIM], kv[..., MLA_NOPE_DIM:]
    k_r = apply_rope(k_rope, cos, sin)
    nblk = S // Q_BLOCK
    scale = (MLA_NOPE_DIM + MLA_ROPE_DIM) ** -0.5
    qn = q_nope.reshape(B, nblk, Q_BLOCK, H, MLA_NOPE_DIM).transpose(1, 0, 3, 2, 4)
    qr = q_rope.reshape(B, nblk, Q_BLOCK, H, MLA_ROPE_DIM).transpose(1, 0, 3, 2, 4)

    def one_block(args):
        qn_b, qr_b = args
        s = (jnp.einsum('bhqd,bkhd->bhqk', qn_b, k_nope)
             + jnp.einsum('bhqr,bkr->bhqk', qr_b, k_r))
        p = jax.nn.softmax(s.astype(jnp.float32) * scale, axis=-1).astype(v.dtype)
        return jnp.einsum('bhqk,bkhd->bqhd', p, v)

    out = lax.map(one_block, (qn, qr))
    return out.transpose(1, 0, 2, 3, 4).reshape(B, S, H * MLA_V_DIM)


def setup_inputs(seed: int = 0) -> dict:
    key = jax.random.key(seed)
    ks = jax.random.split(key, 24)
    L, D = DEPTH, D_MODEL

    def w(k, shape, fan_in):
        return jax.random.normal(k, shape, jnp.float32) * (fan_in ** -0.5)

    def gain(k, shape):
        return 1.0 + 0.05 * jax.random.normal(k, shape, jnp.float32)

    x = jax.random.normal(ks[0], (BATCH, SEQ, D), jnp.float32)
    c = jax.random.normal(ks[1], (BATCH, D), jnp.float32)
    offset = jax.random.randint(ks[2], (BATCH, 1), 0, 1024, dtype=jnp.int32)
    positions = offset + jnp.arange(SEQ, dtype=jnp.int32)[None, :]
    return {
        "x": x,
        "c": c,
        "positions": positions,
        "w_ada": w(ks[3], (L, D, 6 * D), D),
        "b_ada": 0.02 * jax.random.normal(ks[4], (L, 6 * D), jnp.float32),
        "g_mix": gain(ks[5], (L, D)),
        "w_in": w(ks[6], (L, D, IN_COLS), D),
        "b_gate": 0.02 * jax.random.normal(ks[7], (L, 2 * D), jnp.float32),
        "rpb": 0.05 * jax.random.normal(ks[8], (L, NA_HEADS, RPB_ROWS, RPB_COLS), jnp.float32),
        "g_q": gain(ks[9], (L, MLA_Q_RANK)),
        "w_uq": w(ks[10], (L, MLA_Q_RANK, MLA_HEADS * (MLA_NOPE_DIM + MLA_ROPE_DIM)), MLA_Q_RANK),
        "g_kv": gain(ks[11], (L, MLA_KV_RANK)),
        "w_ukv": w(ks[12], (L, MLA_KV_RANK, MLA_HEADS * (MLA_NOPE_DIM + MLA_V_DIM)), MLA_KV_RANK),
        "w_br_na": w(ks[13], (L, NA_WIDTH, D), NA_WIDTH),
        "w_br_mla": w(ks[14], (L, MLA_WIDTH, D), MLA_WIDTH),
        "w_out": w(ks[15], (L, D, D), D),
        "g_mlp": gain(ks[16], (L, D)),
        "w_ff1": w(ks[17], (L, D, D_FF), D),
        "w_ff2": w(ks[18], (L, D_FF, D), D_FF),
        "g_final": gain(ks[19], (D,)),
    }


def reference(x, c, positions, w_ada, b_ada, g_mix, w_in, b_gate, rpb, g_q, w_uq, g_kv, w_ukv,
              w_br_na, w_br_mla, w_out, g_mlp, w_ff1, w_ff2, g_final):
    B, S, D = x.shape
    cos, sin = rope_tables(positions, x.dtype)
    c_act = jax.nn.silu(c)
    for l in range(DEPTH):
        mod = c_act @ w_ada[l] + b_ada[l]
        shift_a, scale_a, gate_a, shift_m, scale_m, gate_m = jnp.split(mod, 6, axis=-1)

        h = rmsnorm(x, g_mix[l]) * (1.0 + scale_a[:, None]) + shift_a[:, None]
        proj = h @ w_in[l]
        q_na, k_na, v_na, c_q, c_kv, k_rope, gate_logits = split_cols(proj, IN_SIZES)
        hd = (B, S, NA_HEADS, NA_HEAD_DIM)
        o_na = neighborhood_attention(q_na.reshape(hd), k_na.reshape(hd), v_na.reshape(hd), rpb[l])
        o_mla = mla_attention(c_q, c_kv, k_rope, w_uq[l], w_ukv[l], g_q[l], g_kv[l], cos, sin)
        gates = jax.nn.sigmoid((gate_logits + b_gate[l]).astype(jnp.float32)).astype(x.dtype)
        g_na, g_mla = gates[..., :D], gates[..., D:]
        merged = g_na * (o_na @ w_br_na[l]) + g_mla * (o_mla @ w_br_mla[l])
        x = x + gate_a[:, None] * (merged @ w_out[l])

        h = rmsnorm(x, g_mlp[l]) * (1.0 + scale_m[:, None]) + shift_m[:, None]
        x = x + gate_m[:, None] * (jnp.square(jax.nn.relu(h @ w_ff1[l])) @ w_ff2[l])
    return rmsnorm(x, g_final)
```

```python
import numpy as np
import concourse.bass as bass
import concourse.mybir as mybir
from concourse.bass_utils import run_bass_kernel_spmd

F32 = mybir.dt.float32
BF16 = mybir.dt.bfloat16
I32 = mybir.dt.int32
ALU = mybir.AluOpType
AF = mybir.ActivationFunctionType

ENGS = ("pe", "act", "dve", "pool", "sp")


class Buf:
    __slots__ = ("name", "w", "r", "excl")

    def __init__(self, name, excl=False):
        self.name = name
        self.w = {}
        self.r = {}
        self.excl = excl


class _Rec:
    def __getattr__(self, name):
        def f(*a, **k):
            self.call = (name, a, k)
            return self
        return f


class Sched:
    def __init__(self, nc):
        self.nc = nc
        self.q = {e: [] for e in ENGS}
        self.cnt = {e: 0 for e in ENGS}
        self.sem = {}
        for e in ("pe", "act", "dve", "pool"):
            self.sem[e] = nc.alloc_semaphore("prog_" + e)
        self.known = {e: {} for e in ENGS}
        self.hist = {e: {} for e in ENGS}
        self.semeng = {id(self.sem[e]): e for e in self.sem}
        self.dsem = {}
        self.nwait = 0

    def _need(self, eng, ev):
        sem, val = ev
        k = self.known[eng]
        if k.get(id(sem), 0) >= val:
            return
        self.q[eng].append(("w", sem, val))
        self.nwait += 1
        k[id(sem)] = val
        src = self.semeng.get(id(sem))
        if src is not None:
            snap = self.hist[src].get(val)
            if snap:
                for s, v in snap.items():
                    if k.get(s, 0) < v:
                        k[s] = v

    def _deps(self, eng, reads, writes):
        own = id(self.sem[eng]) if eng in self.sem else None
        for b in reads:
            for sid, ev in b.w.items():
                self._need(eng, ev)
            if b.excl:
                for sid, ev in b.r.items():
                    if sid != own:
                        self._need(eng, ev)
        for b in writes:
            for sid, ev in b.w.items():
                if sid == own:
                    continue
                self._need(eng, ev)
            for sid, ev in b.r.items():
                if sid == own:
                    continue
                self._need(eng, ev)

    def _mark(self, ev, reads, writes):
        sem, val = ev
        for b in writes:
            b.w = {id(sem): ev}
            b.r = {}
        for b in reads:
            b.r[id(sem)] = ev

    def op(self, eng, fn, reads=(), writes=()):
        rec = _Rec()
        fn(rec)
        call = rec.call
        fn = lambda e, call=call: getattr(e, call[0])(*call[1], **call[2])
        self._deps(eng, reads, writes)
        self.cnt[eng] += 1
        seq = self.cnt[eng]
        sem = self.sem[eng]
        self.q[eng].append(("o", fn, sem, 1))
        snap = dict(self.known[eng])
        snap[id(sem)] = seq - 1
        self.hist[eng][seq] = snap
        self._mark((sem, seq), reads, writes)

    def dma(self, eng, key, out, in_, reads=(), writes=(), transpose=False):
        self._deps(eng, reads, writes)
        if key not in self.dsem:
            self.dsem[key] = [self.nc.alloc_semaphore("d_" + key), 0]
        ent = self.dsem[key]
        ent[1] += 16
        if transpose:
            fn = lambda e, o=out, i=in_: e.dma_start_transpose(out=o, in_=i)
        else:
            fn = lambda e, o=out, i=in_: e.dma_start(out=o, in_=i)
        self.q[eng].append(("o", fn, ent[0], 16))
        self._mark((ent[0], ent[1]), reads, writes)

    def wait_all(self, eng, bufs):
        for b in bufs:
            for ev in list(b.w.values()):
                self._need(eng, ev)

    def emit(self):
        nc = self.nc
        emap = {"pe": "tensor", "act": "scalar", "dve": "vector", "pool": "gpsimd", "sp": "sync"}
        with nc.Block() as block:
            for e in ENGS:
                items = self.q[e]

                def body(engine, items=items):
                    for it in items:
                        if it[0] == "w":
                            engine.wait_ge(it[1], it[2])
                        else:
                            it[1](engine).then_inc(it[2], it[3])

                getattr(block, emap[e])(body)


L_DEPTH = 4
D = 1024
S_TOK = 4096
TT = 512
NT = S_TOK // TT
DC = D // 128
H = 8
NA_SCALE = 64 ** -0.5
MLA_SCALE = 96 ** -0.5
EPS = 1e-6
NPAT = 7
NAB_W = NPAT * 640
MASK_VAL = -200.0
TWO_PI = 2.0 * np.pi
C1 = 6.28125
C2 = TWO_PI - 6.28125
MAGIC = 12582912.0


def na_chunks(i):
    j0 = min(max(i - 2, 0), 27)
    return [j0 + s for s in range(5)]


def na_pat(i):
    if i <= 1:
        return i
    if i >= 30:
        return i - 25
    return 2 + min(max(i - 2, 0), 2) if False else (2 if i < 30 else 0)


def na_pattern_index(i):
    if i == 0:
        return 0
    if i == 1:
        return 1
    if i == 30:
        return 3
    if i == 31:
        return 4
    return 2


NA_REP = [0, 1, 10, 30, 31, 10, 10]


def build_na_index():
    idx_r = np.zeros((NPAT, 128, 640), np.int64)
    idx_c = np.zeros((NPAT, 128, 640), np.int64)
    msk = np.zeros((NPAT, 128, 640), bool)
    rows = 64
    for p in range(NPAT):
        i = NA_REP[p]
        J = na_chunks(i)
        for s in range(5):
            for a in range(2):
                kr = 2 * J[s] + a
                for b in range(2):
                    r = 2 * i + b
                    rs = min(max(r - 4, 0), rows - 8)
                    vrow = (rs <= kr < rs + 8)
                    for kc in range(64):
                        qc = np.arange(64)
                        cs = np.clip(qc - 8, 0, 64 - 16)
                        v = (cs <= kc) & (kc < cs + 16) & vrow
                        dr = kr - r + 7
                        dc = kc - qc + 15
                        key = a * 64 + kc
                        cols = s * 128 + b * 64 + qc
                        msk[p, key, cols] = v
                        idx_r[p, key, cols] = np.clip(dr, 0, 14)
                        idx_c[p, key, cols] = np.clip(dc, 0, 30)
    return idx_r, idx_c, msk


class T:
    __slots__ = ("h", "b")

    def __init__(self, h, name):
        self.h = h
        self.b = Buf(name)


class Ring:
    def __init__(self, items):
        self.items = items
        self.i = 0

    def next(self):
        it = self.items[self.i % len(self.items)]
        self.i += 1
        return it


def build_program(n_layers=L_DEPTH, debug=False, stop_after=None):
    nc = bass.Bass("TRN2", target_bir_lowering=False)
    S = Sched(nc)
    Ln = n_layers

    def din(name, shape, dt=F32):
        return nc.dram_tensor(name, list(shape), dt, kind="ExternalInput").ap()

    def dscr(name, shape, dt):
        kind = "ExternalOutput" if debug else "Internal"
        return nc.dram_tensor(name, list(shape), dt, kind=kind).ap()

    xT_in = din("xT", [D, S_TOK])
    cT_in = din("cT", [128, 8])
    pos_in = din("pos", [1, S_TOK], I32)
    cst_in = din("cst", [128, 4])
    w_ada = din("w_ada", [L_DEPTH, D, 6 * D])
    b_adaT = din("b_adaT", [128, L_DEPTH * 48])
    g_mixT = din("g_mixT", [128, L_DEPTH * 8])
    g_mlpT = din("g_mlpT", [128, L_DEPTH * 8])
    g_finT = din("g_finT", [128, 8])
    w_inA = din("w_inA", [L_DEPTH, D, 1984])
    w_gate = din("w_gate", [L_DEPTH, D, 2048])
    b_gateT = din("b_gateT", [128, L_DEPTH * 16])
    g_qT = din("g_qT", [128, L_DEPTH * 2])
    g_kvT = din("g_kvT", [128, L_DEPTH * 1])
    w_uqp = din("w_uqp", [L_DEPTH, 256, 1024])
    w_uk = din("w_uk", [L_DEPTH, 128, 512])
    w_uv = din("w_uv", [L_DEPTH, 128, 512])
    w_brna = din("w_br_na", [L_DEPTH, 512, D])
    w_brml = din("w_br_mla", [L_DEPTH, 512, D])
    w_out = din("w_out", [L_DEPTH, D, D])
    w_ff1 = din("w_ff1", [L_DEPTH, D, 4 * D])
    w_ff2 = din("w_ff2", [L_DEPTH, 4 * D, D])
    nab_in = din("nab", [L_DEPTH, H, 128, NAB_W])
    outT = nc.dram_tensor("outT", [D, S_TOK], F32, kind="ExternalOutput").ap()

    xT_d = dscr("xT_d", [D, S_TOK], F32)
    hT_d = dscr("hT_d", [D, S_TOK], BF16)
    qna_d = dscr("qna_d", [512, S_TOK], BF16)
    kna_d = dscr("kna_d", [512, S_TOK], BF16)
    vna_d = dscr("vna_d", [128, 32, H, 65], BF16)
    qm_d = dscr("qm_d", [H, 96, S_TOK], BF16)
    km_d = dscr("km_d", [H, 64, S_TOK], BF16)
    kr_d = dscr("kr_d", [32, S_TOK], BF16)
    vm_d = dscr("vm_d", [128, 32, H, 65], BF16)
    oT_d = dscr("oT_d", [2, 512, S_TOK], BF16)
    cos_d = dscr("cos_d", [128, S_TOK], F32)
    sin_d = dscr("sin_d", [128, S_TOK], F32)
    recd_d = dscr("recd_d", [4, 512], F32)
    dram_b = Buf("dram")

    SB_BASE = 16512
    SB_LIMIT = 206 * 1024
    arena = {"off": 0}

    def sb(name, shape, dt, off=None):
        esz = 4 if dt in (F32, I32) else 2
        nbytes = int(np.prod(shape[1:])) * esz
        nbytes = (nbytes + 31) // 32 * 32
        if off is None:
            off = arena["off"]
            arena["off"] = off + nbytes
        assert off + nbytes <= SB_LIMIT, (name, off, nbytes)
        h = nc.alloc_sbuf_tensor_at(name, list(shape), dt, offset=SB_BASE + off)
        return T(h, name)

    class Arena:
        def __init__(self, base, limit):
            self.base = base
            self.off = base
            self.limit = limit

        def reset(self):
            self.off = self.base

        def alloc(self, name, shape, dt):
            esz = 4 if dt in (F32, I32) else 2
            nbytes = int(np.prod(shape[1:])) * esz
            nbytes = (nbytes + 31) // 32 * 32
            assert self.off + nbytes <= self.limit, (name, self.off, nbytes, self.limit)
            t = sb(name, shape, dt, off=self.off)
            self.off += nbytes
            return t

    CONST = Arena(0, 6 * 1024)
    ones_bf = CONST.alloc("ones_bf", [128, 128], BF16)
    ones_f = CONST.alloc("ones_f", [128, 64], F32)
    modv = CONST.alloc("modv", [128, L_DEPTH * 48], F32)
    gsA = CONST.alloc("gsA", [128, L_DEPTH * 8], F32)
    gsM = CONST.alloc("gsM", [128, L_DEPTH * 8], F32)
    gmix = CONST.alloc("gmix", [128, L_DEPTH * 8], F32)
    gmlp = CONST.alloc("gmlp", [128, L_DEPTH * 8], F32)
    gfin = CONST.alloc("gfin", [128, 8], F32)
    bgate = CONST.alloc("bgate", [128, L_DEPTH * 16], F32)
    gq = CONST.alloc("gq", [128, L_DEPTH * 2], F32)
    gkv = CONST.alloc("gkv", [128, L_DEPTH], F32)
    cst = CONST.alloc("cst", [128, 4], F32)
    cact = CONST.alloc("cact", [128, 8], F32)
    eps_t = CONST.alloc("eps_t", [128, 8], F32)
    WREG = Arena(6 * 1024, 134 * 1024)
    WORK = Arena(134 * 1024, SB_LIMIT)

    ps_all = nc.alloc_psum_tensor("ps_all", [128, 4096], F32)
    bankb = [Buf("bank%d" % i, excl=True) for i in range(8)]

    class PS:
        def __init__(self, b0, nb=1):
            self.b0 = b0
            self.nb = nb
            self.bufs = [bankb[b0 + i] for i in range(nb)]

        def ap(self, p0, p1, c0, c1):
            return ps_all[p0:p1, self.b0 * 512 + c0:self.b0 * 512 + c1]

    ps_ring = Ring([PS(i) for i in range(8)])

    uid = {"n": 0}

    def barrier():
        evs = []
        for e in ("pe", "act", "dve", "pool"):
            if S.cnt[e] > 0:
                evs.append((S.sem[e], S.cnt[e]))
        for k, ent in S.dsem.items():
            if ent[1] > 0:
                evs.append((ent[0], ent[1]))
        for e in ENGS:
            for ev in evs:
                S._need(e, ev)

    def store(key, out, in_, reads, eng="sp"):
        S.dma(eng, key, out, in_, reads=reads, writes=[])

    def load(key, out, in_, writes, eng="sp"):
        S.dma(eng, key, out, in_, reads=[], writes=writes)

    def phase0():
        WREG.reset()
        WORK.reset()
        S.op("dve", lambda e: e.memset(ones_bf.h[:, :], 1.0), writes=[ones_bf.b])
        S.op("dve", lambda e: e.memset(ones_f.h[:, :], 1.0), writes=[ones_f.b])
        S.op("dve", lambda e: e.memset(eps_t.h[:, :], float(EPS)), writes=[eps_t.b])
        load("cst", cst.h[:, :], cst_in, [cst.b])
        load("cact", cact.h[:, :], cT_in, [cact.b])
        load("gmix", gmix.h[:, :], g_mixT, [gmix.b])
        load("gmlp", gmlp.h[:, :], g_mlpT, [gmlp.b])
        load("gfin", gfin.h[:, :], g_finT, [gfin.b])
        load("bgate", bgate.h[:, :], b_gateT, [bgate.b])
        load("gq", gq.h[:, :], g_qT, [gq.b])
        load("gkv", gkv.h[:, :], g_kvT, [gkv.b])
        S.op("act", lambda e: e.activation(cact.h[:, :], cact.h[:, :], AF.Silu), reads=[cact.b], writes=[cact.b])
        import os
        P0SKIP = os.environ.get("P0SKIP", "")
        posi = WORK.alloc("posi", [128, S_TOK], I32)
        ang = WORK.alloc("ang", [128, S_TOK], F32)
        t1 = WORK.alloc("rt1", [128, S_TOK], F32)
        t2 = WORK.alloc("rt2", [128, S_TOK], F32)
        load("posi", posi.h[:, :], pos_in.partition_broadcast(128), [posi.b])
        S.op("dve", lambda e: e.tensor_copy(ang.h[:, :], posi.h[:, :]), reads=[posi.b], writes=[ang.b])
        S.op("dve", lambda e: e.tensor_scalar(ang.h[:, :], ang.h[:, :], cst.h[:, 0:1], None, ALU.mult),
             reads=[ang.b, cst.b], writes=[ang.b])
        for which, dst in ((0, sin_d), (1, cos_d)):
            if "rope" in P0SKIP:
                break
            if which == 1:
                S.op("dve", lambda e: e.tensor_scalar(ang.h[:, :], ang.h[:, :], float(np.pi / 2), None, ALU.add),
                     reads=[ang.b], writes=[ang.b])
            S.op("dve", lambda e: e.tensor_scalar(t1.h[:, :], ang.h[:, :], float(1.0 / TWO_PI), MAGIC, ALU.mult, ALU.add),
                 reads=[ang.b], writes=[t1.b])
            S.op("dve", lambda e: e.tensor_scalar(t1.h[:, :], t1.h[:, :], -MAGIC, None, ALU.add),
                 reads=[t1.b], writes=[t1.b])
            S.op("dve", lambda e: e.scalar_tensor_tensor(t2.h[:, :], t1.h[:, :], -C1, ang.h[:, :], ALU.mult, ALU.add),
                 reads=[t1.b, ang.b], writes=[t2.b])
            S.op("dve", lambda e: e.scalar_tensor_tensor(t2.h[:, :], t1.h[:, :], -float(C2), t2.h[:, :], ALU.mult, ALU.add),
                 reads=[t1.b, t2.b], writes=[t2.b])
            if which == 0:
                S.op("act", lambda e: e.activation(t2.h[:, :], t2.h[:, :], AF.Sin, scale=cst.h[:, 1:2]),
                     reads=[t2.b, cst.b], writes=[t2.b])
            else:
                S.op("act", lambda e: e.activation(t2.h[:, :], t2.h[:, :], AF.Sin), reads=[t2.b], writes=[t2.b])
            store("st_rope", dst, t2.h[:, :], [t2.b])
        stg = [WREG.alloc("adastg%d" % i, [128, 8, 1024], F32) for i in range(2)]
        badd = WREG.alloc("badd", [128, L_DEPTH * 48], F32)
        modrow = WREG.alloc("modrow", [128, 6 * D], F32)
        load("badd", badd.h[:, :], b_adaT, [badd.b])
        n = 0
        for l in range(Ln):
            if "mod" in P0SKIP:
                break
            for g in range(6):
                st = stg[n % 2]
                n += 1
                load(st.b.name, st.h[:, :, :],
                     w_ada[l, :, g * 1024:(g + 1) * 1024].rearrange("(k p) c -> p k c", p=128), [st.b])
                for hf in range(2):
                    ps = ps_ring.next()
                    for k in range(8):
                        S.op("pe", lambda e: e.matmul(ps.ap(0, 1, 0, 512), cact.h[:, k:k + 1], st.h[:, k, hf * 512:(hf + 1) * 512],
                                                      start=(k == 0), stop=(k == 7)), reads=[st.b, cact.b], writes=ps.bufs)
                    c0 = g * 1024 + hf * 512
                    S.op("act", lambda e: e.activation(modrow.h[0:1, c0:c0 + 512], ps.ap(0, 1, 0, 512), AF.Copy),
                         reads=ps.bufs, writes=[modrow.b])
            psm = ps_ring.next()
            for j in range(48):
                S.op("pe", lambda e: e.matmul(psm.ap(0, 128, j, j + 1), modrow.h[0:1, j * 128:(j + 1) * 128], ones_f.h[0:1, 0:1],
                                              start=True, stop=True), reads=[modrow.b, ones_f.b], writes=psm.bufs)
            S.op("dve", lambda e, l=l, psm=psm: e.tensor_tensor(
                modv.h[:, l * 48:(l + 1) * 48], psm.ap(0, 128, 0, 48), badd.h[:, l * 48:(l + 1) * 48], ALU.add),
                reads=psm.bufs + [badd.b], writes=[modv.b])
            S.op("dve", lambda e, l=l: e.scalar_tensor_tensor(
                gsA.h[:, l * 8:(l + 1) * 8], modv.h[:, l * 48 + 8:l * 48 + 16], 1.0, gmix.h[:, l * 8:(l + 1) * 8],
                ALU.add, ALU.mult), reads=[modv.b, gmix.b], writes=[gsA.b])
            S.op("dve", lambda e, l=l: e.scalar_tensor_tensor(
                gsM.h[:, l * 8:(l + 1) * 8], modv.h[:, l * 48 + 32:l * 48 + 40], 1.0, gmlp.h[:, l * 8:(l + 1) * 8],
                ALU.add, ALU.mult), reads=[modv.b, gmlp.b], writes=[gsM.b])
        barrier()

    def rstd_from_sq(sq_aps, sq_bufs, dim, rstd_t):
        pss = ps_ring.next()
        n = len(sq_aps)
        for k in range(n):
            S.op("pe", lambda e, k=k, pss=pss: e.matmul(pss.ap(0, 128, 0, TT), ones_bf.h[:, :], sq_aps[k],
                                                        start=(k == 0), stop=(k == n - 1)),
                 reads=[ones_bf.b] + sq_bufs, writes=pss.bufs)
        S.op("act", lambda e, pss=pss: e.activation(rstd_t.h[:, :], pss.ap(0, 128, 0, TT), AF.Ln,
                                                    bias=eps_t.h[:, 0:1], scale=float(1.0 / dim)),
             reads=pss.bufs + [eps_t.b], writes=[rstd_t.b])
        S.op("act", lambda e: e.activation(rstd_t.h[:, :], rstd_t.h[:, :], AF.Exp, scale=-0.5),
             reads=[rstd_t.b], writes=[rstd_t.b])

    ps_ring7 = Ring([PS(i) for i in range(7)])
    ps_ss = PS(7)

    def rstd_finish(pss, dim, rstd_t):
        S.op("act", lambda e: e.activation(rstd_t.h[:, :], pss.ap(0, 128, 0, TT), AF.Ln,
                                           bias=eps_t.h[:, 0:1], scale=float(1.0 / dim)),
             reads=pss.bufs + [eps_t.b], writes=[rstd_t.b])
        S.op("act", lambda e: e.activation(rstd_t.h[:, :], rstd_t.h[:, :], AF.Exp, scale=-0.5),
             reads=[rstd_t.b], writes=[rstd_t.b])

    def resid_norm_loop(xt, hbuf, gate_col, mm_group, ring):
        def ss_mm(k):
            S.op("pe", lambda e: e.matmul(ps_ss.ap(0, 128, 0, TT), ones_bf.h[:, :], hbuf.h[:, k, :],
                                          start=(k == 0), stop=(k == 7)), reads=[ones_bf.b, hbuf.b], writes=ps_ss.bufs)
        for c2 in range(8):
            ps = ring.next()
            mm_group(c2, ps)
            if c2 >= 1:
                ss_mm(c2 - 1)
            S.op("dve", lambda e: e.scalar_tensor_tensor(xt.h[:, c2, :], ps.ap(0, 128, 0, TT),
                                                         modv.h[:, gate_col + c2:gate_col + c2 + 1], xt.h[:, c2, :],
                                                         ALU.mult, ALU.add), reads=ps.bufs + [modv.b, xt.b], writes=[xt.b])
            S.op("act", lambda e: e.activation(hbuf.h[:, c2, :], xt.h[:, c2, :], AF.Square), reads=[xt.b], writes=[hbuf.b])
        ss_mm(7)

    def norm_tile(xt, hout, hout_b, gs_t, gcol, shcol, sq_all, sq_k, sq_b, rstd_t, tmp_ring):
        S.op("act", lambda e: e.activation(sq_all, xt.h[:, :, :], AF.Square), reads=[xt.b], writes=[sq_b])
        rstd_from_sq([sq_k(k) for k in range(8)], [sq_b], D, rstd_t)
        for k in range(8):
            tmp = tmp_ring.next()
            S.op("dve", lambda e: e.tensor_tensor(tmp.h[:, :], xt.h[:, k, :], rstd_t.h[:, :], ALU.mult),
                 reads=[xt.b, rstd_t.b], writes=[tmp.b])
            S.op("act", lambda e: e.activation(
                hout(k), tmp.h[:, :], AF.Identity, bias=modv.h[:, shcol + k:shcol + k + 1],
                scale=gs_t.h[:, gcol + k:gcol + k + 1]), reads=[tmp.b, modv.b, gs_t.b], writes=[hout_b])

    def load_weight_bf16(dst_t, dst_ap, src_ap):
        S.dma("pool", dst_t.b.name, dst_ap, src_ap, reads=[], writes=[dst_t.b])

    def phase1(l):
        WREG.reset()
        WORK.reset()
        x_src = xT_in if l == 0 else xT_d
        wA = WREG.alloc("wA", [128, 8, 1984], BF16)
        wq = WREG.alloc("wq", [128, 2, 1024], BF16)
        wk = WREG.alloc("wk", [128, 512], BF16)
        wv = WREG.alloc("wv", [128, 512], BF16)
        wA_grp = [(0, 512), (512, 1024), (1024, 1536), (1536, 1984)]
        wA_b = [Buf("wA_g%d" % i) for i in range(4)]
        for gi in (3, 0, 1, 2):
            a, b = wA_grp[gi]
            S.dma("pool", "wA_g%d" % gi, wA.h[:, :, a:b], w_inA[l, :, a:b].rearrange("(k p) c -> p k c", p=128),
                  reads=[], writes=[wA_b[gi]])

        def wAb(col0):
            for gi, (a, b) in enumerate(wA_grp):
                if a <= col0 < b:
                    return wA_b[gi]
        S.dma("pool", "wq", wq.h[:, :, :], w_uqp[l].rearrange("(k p) c -> p k c", p=128), reads=[], writes=[wq.b])
        S.dma("pool", "wk", wk.h[:, :], w_uk[l], reads=[], writes=[wk.b])
        S.dma("pool", "wv", wv.h[:, :], w_uv[l], reads=[], writes=[wv.b])
        hTs = Ring([WORK.alloc("p1h%d" % i, [128, 8, TT], BF16) for i in range(2)])
        tmp_ring = Ring([WORK.alloc("p1tmp%d" % i, [128, TT], F32) for i in range(3)])
        if l == 0:
            xt0 = WORK.alloc("p1xt", [128, 8, TT], F32)
            sq = WORK.alloc("p1sq", [128, 8, TT], BF16)
            rstd_t = WORK.alloc("p1rstd", [128, TT], F32)
        stg = Ring([WORK.alloc("p1stg%d" % i, [128, TT], BF16) for i in range(3)])
        vstg = Ring([WORK.alloc("p1vstg%d" % i, [128, H, 65], BF16) for i in range(2)])
        for vs in vstg.items:
            S.op("dve", lambda e: e.memset(vs.h[:, :, 64:65], 1.0), writes=[vs.b])
        cs_t = Ring([WORK.alloc("p1cs%d" % i, [128, 2, TT], F32) for i in range(1)])
        cq_f = WORK.alloc("p1cqf", [128, 3, TT], F32)
        cq_sq = WORK.alloc("p1cqsq", [128, 3, TT], BF16)
        cqn = WORK.alloc("p1cqn", [128, 3, TT], BF16)
        rs_q = WORK.alloc("p1rsq", [128, TT], F32)
        rt = tmp_ring
        import os
        P1SKIP = os.environ.get("P1SKIP", "")
        NTL = int(os.environ.get("P1NT", NT))
        hmap = {}

        def get_h(t):
            a0, a1 = t * TT, (t + 1) * TT
            hT = hTs.next()
            hmap[t] = hT
            if l == 0:
                load("p1xt", xt0.h[:, :, :], xT_in[:, a0:a1].rearrange("(k p) t -> p k t", p=128), [xt0.b])
                norm_tile(xt0, lambda k: hT.h[:, k, :], hT.b, gsA, l * 8, l * 48 + 0,
                          sq.h[:, :, :], lambda k: sq.h[:, k, :], sq.b, rstd_t, tmp_ring)
                store("st_h", hT_d[:, a0:a1].rearrange("(k p) t -> p k t", p=128), hT.h[:, :, :], [hT.b])
            else:
                load(hT.b.name, hT.h[:, :, :], hT_d[:, a0:a1].rearrange("(k p) t -> p k t", p=128), [hT.b])

        get_h(0)
        for t in range(NTL):
            c0, c1 = t * TT, (t + 1) * TT
            hT = hmap.pop(t)
            cs = cs_t.next()
            load(cs.b.name + "c", cs.h[:, 0, :], cos_d[:, c0:c1], [cs.b])
            load(cs.b.name + "s", cs.h[:, 1, :], sin_d[:, c0:c1], [cs.b])

            def proj_fm(col0, m, evac):
                ps = ps_ring.next()
                for k in range(8):
                    S.op("pe", lambda e, k=k, ps=ps: e.matmul(ps.ap(0, m, 0, TT), wA.h[:, k, col0:col0 + m], hT.h[:, k, :],
                                                              start=(k == 0), stop=(k == 7)),
                         reads=[wAb(col0), hT.b], writes=ps.bufs)
                evac(ps)

            if "all" in P1SKIP:
                continue
            for j in range(3):
                def ev(ps, j=j):
                    S.op("dve", lambda e, ps=ps: e.tensor_copy(cq_f.h[:, j, :], ps.ap(0, 128, 0, TT)),
                         reads=ps.bufs, writes=[cq_f.b])
                    S.op("act", lambda e, ps=ps: e.activation(cq_sq.h[:, j, :], cq_f.h[:, j, :], AF.Square),
                         reads=[cq_f.b], writes=[cq_sq.b])
                proj_fm(1536 + j * 128, 128, ev)
            psa = ps_ring.next()
            psb = ps_ring.next()
            for (ps, col0) in ((psa, 1920), (psb, 1952)):
                for k in range(8):
                    S.op("pe", lambda e, k=k, ps=ps, col0=col0: e.matmul(
                        ps.ap(0, 32, 0, TT), wA.h[:, k, col0:col0 + 32], hT.h[:, k, :], start=(k == 0), stop=(k == 7)),
                        reads=[wAb(col0), hT.b], writes=ps.bufs)
            ta, tb = rt.next(), rt.next()
            S.op("dve", lambda e: e.tensor_tensor(ta.h[0:32, :], psa.ap(0, 32, 0, TT), cs.h[0:32, 0, :], ALU.mult),
                 reads=psa.bufs + [cs.b], writes=[ta.b])
            S.op("dve", lambda e: e.tensor_tensor(tb.h[0:32, :], psb.ap(0, 32, 0, TT), cs.h[0:32, 1, :], ALU.mult),
                 reads=psb.bufs + [cs.b], writes=[tb.b])
            st = stg.next()
            S.op("dve", lambda e, st=st: e.tensor_tensor(st.h[0:32, :], ta.h[0:32, :], tb.h[0:32, :], ALU.add),
                 reads=[ta.b, tb.b], writes=[st.b])
            store("st_" + st.b.name, kr_d[:, c0:c1], st.h[0:32, :], [st.b])
            for cch in range(0 if "qk" not in P1SKIP else 8, 8):
                def ev(ps, cch=cch):
                    st = stg.next()
                    S.op("act", lambda e, ps=ps, st=st: e.activation(st.h[:, :], ps.ap(0, 128, 0, TT), AF.Copy),
                         reads=ps.bufs, writes=[st.b])
                    dst = qna_d if cch < 4 else kna_d
                    r0 = (cch % 4) * 128
                    store("st_" + st.b.name, dst[r0:r0 + 128, c0:c1], st.h[:, :], [st.b])
                proj_fm(cch * 128, 128, ev)
            rstd_from_sq([cq_sq.h[:, 0, :], cq_sq.h[:, 1, :]], [cq_sq.b], 256, rs_q)
            rs_kv = tmp_ring.next()
            rstd_from_sq([cq_sq.h[:, 2, :]], [cq_sq.b], 128, rs_kv)
            for j in range(3):
                gcol = gq.h[:, l * 2 + j:l * 2 + j + 1] if j < 2 else gkv.h[:, l:l + 1]
                rsx = rs_q if j < 2 else rs_kv
                S.op("dve", lambda e, j=j, gcol=gcol, rsx=rsx: e.scalar_tensor_tensor(
                    cqn.h[:, j, :], cq_f.h[:, j, :], gcol, rsx.h[:, :], ALU.mult, ALU.mult),
                    reads=[cq_f.b, gq.b, gkv.b, rsx.b], writes=[cqn.b])
            if t + 1 < NTL:
                get_h(t + 1)
            for sbk in range(4 if "vna" not in P1SKIP else 0):
                ps = ps_ring.next()
                for k in range(8):
                    S.op("pe", lambda e, k=k, ps=ps, sbk=sbk: e.matmul(
                        ps.ap(0, 128, 0, 512), hT.h[:, k, sbk * 128:(sbk + 1) * 128], wA.h[:, k, 1024:1536],
                        start=(k == 0), stop=(k == 7)), reads=[wAb(1024), hT.b], writes=ps.bufs)
                st = vstg.next()
                S.op("dve", lambda e, ps=ps, st=st: e.tensor_copy(st.h[:, :, 0:64], ps.ap(0, 128, 0, 512).rearrange("p (h d) -> p h d", h=H)),
                     reads=ps.bufs, writes=[st.b])
                cidx = t * 4 + sbk
                store("st_" + st.b.name, vna_d[:, cidx, :, :], st.h[:, :, :], [st.b])
            if "qn" in P1SKIP:
                continue
            for g in range(4):
                ps = ps_ring.next()
                for j in range(2):
                    S.op("pe", lambda e, j=j, ps=ps, g=g: e.matmul(
                        ps.ap(0, 128, 0, TT), wq.h[:, j, g * 128:(g + 1) * 128], cqn.h[:, j, :],
                        start=(j == 0), stop=(j == 1)), reads=[wq.b, cqn.b], writes=ps.bufs)
                st = stg.next()
                S.op("act", lambda e, ps=ps, st=st: e.activation(st.h[:, :], ps.ap(0, 128, 0, TT), AF.Copy),
                     reads=ps.bufs, writes=[st.b])
                for hh in range(2):
                    store("st_" + st.b.name, qm_d[2 * g + hh, 0:64, c0:c1], st.h[hh * 64:(hh + 1) * 64, :], [st.b])
            if "qr" in P1SKIP:
                continue
            for g in range(2):
                psa = ps_ring.next()
                psb = ps_ring.next()
                for (ps, col0) in ((psa, 512 + g * 128), (psb, 768 + g * 128)):
                    for j in range(2):
                        S.op("pe", lambda e, j=j, ps=ps, col0=col0: e.matmul(
                            ps.ap(0, 128, 0, TT), wq.h[:, j, col0:col0 + 128], cqn.h[:, j, :],
                            start=(j == 0), stop=(j == 1)), reads=[wq.b, cqn.b], writes=ps.bufs)
                ta, tb = rt.next(), rt.next()
                S.op("dve", lambda e, psa=psa, ta=ta: e.tensor_tensor(ta.h[:, :], psa.ap(0, 128, 0, TT), cs.h[:, 0, :], ALU.mult),
                     reads=psa.bufs + [cs.b], writes=[ta.b])
                S.op("dve", lambda e, psb=psb, tb=tb: e.tensor_tensor(tb.h[:, :], psb.ap(0, 128, 0, TT), cs.h[:, 1, :], ALU.mult),
                     reads=psb.bufs + [cs.b], writes=[tb.b])
                st = stg.next()
                S.op("dve", lambda e, st=st, ta=ta, tb=tb: e.tensor_tensor(st.h[:, :], ta.h[:, :], tb.h[:, :], ALU.add),
                     reads=[ta.b, tb.b], writes=[st.b])
                for hh in range(4):
                    store("st_" + st.b.name, qm_d[4 * g + hh, 64:96, c0:c1], st.h[hh * 32:(hh + 1) * 32, :], [st.b])
            if "kn" in P1SKIP:
                continue
            for g in range(4):
                ps = ps_ring.next()
                S.op("pe", lambda e, ps=ps, g=g: e.matmul(ps.ap(0, 128, 0, TT), wk.h[:, g * 128:(g + 1) * 128], cqn.h[:, 2, :],
                                                         start=True, stop=True), reads=[wk.b, cqn.b], writes=ps.bufs)
                st = stg.next()
                S.op("act", lambda e, ps=ps, st=st: e.activation(st.h[:, :], ps.ap(0, 128, 0, TT), AF.Copy),
                     reads=ps.bufs, writes=[st.b])
                for hh in range(2):
                    store("st_" + st.b.name, km_d[2 * g + hh, :, c0:c1], st.h[hh * 64:(hh + 1) * 64, :], [st.b])
            for sbk in range(4):
                ps = ps_ring.next()
                S.op("pe", lambda e, ps=ps, sbk=sbk: e.matmul(ps.ap(0, 128, 0, 512), cqn.h[:, 2, sbk * 128:(sbk + 1) * 128],
                                                             wv.h[:, :], start=True, stop=True),
                     reads=[wv.b, cqn.b], writes=ps.bufs)
                st = vstg.next()
                S.op("dve", lambda e, ps=ps, st=st: e.tensor_copy(st.h[:, :, 0:64], ps.ap(0, 128, 0, 512).rearrange("p (h d) -> p h d", h=H)),
                     reads=ps.bufs, writes=[st.b])
                cidx = t * 4 + sbk
                store("st_" + st.b.name, vm_d[:, cidx, :, :], st.h[:, :, :], [st.b])
        barrier()

    def load_w3(l):
        A = Arena(142 * 1024, SB_LIMIT)
        wg = A.alloc("wg", [128, 8, 2048], BF16)
        wbn = A.alloc("wbn", [128, 4, 1024], BF16)
        wbm = A.alloc("wbm", [128, 4, 1024], BF16)
        wo = A.alloc("wo", [128, 8, 1024], BF16)
        for k in range(8):
            S.dma("pool", "wg", wg.h[:, k, :], w_gate[l, k * 128:(k + 1) * 128, :], reads=[], writes=[wg.b])
        S.dma("pool", "wbn", wbn.h[:, :, :], w_brna[l].rearrange("(k p) c -> p k c", p=128), reads=[], writes=[wbn.b])
        S.dma("pool", "wbm", wbm.h[:, :, :], w_brml[l].rearrange("(k p) c -> p k c", p=128), reads=[], writes=[wbm.b])
        for k in range(0, 8, 4):
            S.dma("pool", "wo", wo.h[:, k:k + 4, :], w_out[l, k * 128:(k + 4) * 128, :].rearrange("(k p) c -> p k c", p=128),
                  reads=[], writes=[wo.b])
        return wg, wbn, wbm, wo

    def load_w1ff(l):
        A = Arena(78 * 1024, 142 * 1024)
        w1 = A.alloc("w1", [128, 8, 4096], BF16)
        for k in range(8):
            S.dma("pool", "w1", w1.h[:, k, :], w_ff1[l, k * 128:(k + 1) * 128, :], reads=[], writes=[w1.b])
        return w1

    def phase2(l):
        WREG.reset()
        WORK.reset()
        A2 = Arena(WREG.base, 142 * 1024)
        na_slots = []
        ml_slots = []
        for i in range(2):
            na_slots.append(dict(
                q=A2.alloc("naq%d" % i, [128, S_TOK], BF16), k=A2.alloc("nak%d" % i, [128, S_TOK], BF16),
                v=A2.alloc("nav%d" % i, [128, 32, 65], BF16), b=A2.alloc("nab%d" % i, [128, NAB_W], BF16)))
            ml_slots.append(dict(
                q=A2.alloc("mlq%d" % i, [128, S_TOK], BF16), k=A2.alloc("mlk%d" % i, [128, S_TOK], BF16),
                v=A2.alloc("mlv%d" % i, [128, 32, 65], BF16)))
        Tt = Ring([A2.alloc("naT%d" % i, [128, 640], F32) for i in range(3)])
        Pn = Ring([A2.alloc("naP%d" % i, [128, 640], BF16) for i in range(3)])
        Pm = Ring([A2.alloc("mlP%d" % i, [128, 1024], BF16) for i in range(4)])
        recs = Ring([A2.alloc("rec%d" % i, [128, 512], F32) for i in range(3)])
        bcs = Ring([A2.alloc("bc%d" % i, [128, 512], F32) for i in range(3)])
        osts = Ring([A2.alloc("ost%d" % i, [128, 512], BF16) for i in range(2)])
        psS_n = Ring([PS(0, 2), PS(2, 2), PS(4, 2)])
        psO_r = Ring([PS(6), PS(7)])
        psB_r = psS_n

        def loads(h):
            n = na_slots[h % 2]
            m = ml_slots[h % 2]
            load(n["q"].b.name, n["q"].h[0:64, :], qna_d[h * 64:(h + 1) * 64, :], [n["q"].b])
            load(n["k"].b.name, n["k"].h[0:64, :], kna_d[h * 64:(h + 1) * 64, :], [n["k"].b])
            load(n["v"].b.name, n["v"].h[:, :, :], vna_d[:, :, h, :], [n["v"].b])
            S.dma("pool", n["b"].b.name, n["b"].h[:, :], nab_in[l, h], reads=[], writes=[n["b"].b])
            load(m["q"].b.name, m["q"].h[0:96, :], qm_d[h], [m["q"].b])
            load(m["k"].b.name + "a", m["k"].h[0:64, :], km_d[h], [m["k"].b])
            load(m["k"].b.name + "b", m["k"].h[64:96, :], kr_d, [m["k"].b])
            load(m["v"].b.name, m["v"].h[:, :, :], vm_d[:, :, h, :], [m["v"].b])

        pending = []
        recd_state = {"i": 0}
        recd_b = [Buf("recd%d" % i) for i in range(4)]

        def normalize(psO, branch, h, q0):
            rec = recs.next()
            if branch == 0:
                S.op("act", lambda e: e.activation(rec.h[64:65, :], psO.ap(64, 65, 0, 512), AF.Ln), reads=psO.bufs, writes=[rec.b])
                S.op("act", lambda e: e.activation(rec.h[64:65, :], rec.h[64:65, :], AF.Exp, scale=-1.0), reads=[rec.b], writes=[rec.b])
            else:
                S.op("dve", lambda e: e.reciprocal(rec.h[64:65, :], psO.ap(64, 65, 0, 512)), reads=psO.bufs, writes=[rec.b])

            if branch == 1:
                slot = recd_state["i"] % 4
                recd_state["i"] += 1
                bc = bcs.next()
                S.dma("sp", "recd%d" % slot, recd_d[slot:slot + 1, :], rec.h[64:65, :], reads=[rec.b], writes=[recd_b[slot]])
                S.dma("sp", bc.b.name, bc.h[0:64, :], recd_d[slot:slot + 1, :].partition_broadcast(64),
                      reads=[recd_b[slot]], writes=[bc.b])

                def part_b():
                    ost = osts.next()
                    S.op("dve", lambda e: e.tensor_tensor(ost.h[0:64, :], psO.ap(0, 64, 0, 512), bc.h[0:64, :], ALU.mult),
                         reads=psO.bufs + [bc.b], writes=[ost.b])
                    store("st_" + ost.b.name, oT_d[branch, h * 64:(h + 1) * 64, q0:q0 + 512], ost.h[0:64, :], [ost.b])
            else:
                def part_b():
                    psB = psB_r.next()
                    bc = bcs.next()
                    ost = osts.next()
                    S.op("pe", lambda e: e.matmul(psB.ap(0, 64, 0, 512), ones_f.h[64:65, 0:64], rec.h[64:65, :],
                                                  start=True, stop=True), reads=[ones_f.b, rec.b], writes=psB.bufs)
                    S.op("act", lambda e: e.activation(bc.h[0:64, :], psB.ap(0, 64, 0, 512), AF.Copy),
                         reads=psB.bufs, writes=[bc.b])
                    S.op("dve", lambda e: e.tensor_tensor(ost.h[0:64, :], psO.ap(0, 64, 0, 512), bc.h[0:64, :], ALU.mult),
                         reads=psO.bufs + [bc.b], writes=[ost.b])
                    store("st_" + ost.b.name, oT_d[branch, h * 64:(h + 1) * 64, q0:q0 + 512], ost.h[0:64, :], [ost.b])
            pending.append(part_b)

        def flush():
            while pending:
                pending.pop(0)()

        def na_head(h):
            n = na_slots[h % 2]
            q, k, v, bt = n["q"], n["k"], n["v"], n["b"]
            state = {}

            def s_mm(i):
                J = na_chunks(i)
                psS = psS_n.next()
                for s in range(5):
                    S.op("pe", lambda e, s=s: e.matmul(psS.ap(0, 128, s * 128, (s + 1) * 128),
                                                       k.h[0:64, J[s] * 128:(J[s] + 1) * 128],
                                                       q.h[0:64, i * 128:(i + 1) * 128], start=True, stop=True),
                         reads=[k.b, q.b], writes=psS.bufs)
                tt = Tt.next()
                pn = Pn.next()
                pat = na_pattern_index(i)
                S.op("dve", lambda e: e.scalar_tensor_tensor(tt.h[:, :], psS.ap(0, 128, 0, 640), float(NA_SCALE),
                                                             bt.h[:, pat * 640:(pat + 1) * 640], ALU.mult, ALU.add),
                     reads=psS.bufs + [bt.b], writes=[tt.b])
                S.op("act", lambda e: e.activation(pn.h[:, :], tt.h[:, :], AF.Exp), reads=[tt.b], writes=[pn.b])
                state[i] = (J, pn)

            def o_mm(i, psO):
                J, pn = state.pop(i)
                ib = i % 4
                for s in range(5):
                    S.op("pe", lambda e, s=s: e.matmul(psO.ap(0, 65, ib * 128, (ib + 1) * 128),
                                                       v.h[:, J[s], 0:65], pn.h[:, s * 128:(s + 1) * 128],
                                                       start=(s == 0), stop=(s == 4)),
                         reads=[v.b, pn.b], writes=psO.bufs)

            s_mm(0)
            s_mm(1)
            psO = None
            for i in range(32):
                if i % 4 == 0:
                    psO = psO_r.next()
                if i + 2 < 32:
                    s_mm(i + 2)
                o_mm(i, psO)
                if i % 4 == 1:
                    flush()
                if i % 4 == 3:
                    normalize(psO, 0, h, (i // 4) * 512)

        def mla_head(h):
            m = ml_slots[h % 2]
            q, k, v = m["q"], m["k"], m["v"]
            items = [(qt, p) for qt in range(8) for p in range(16)]
            state = {}
            psO_of = {}

            def s_mm(idx):
                qt, p = items[idx]
                psS = psS_n.next()
                for j in range(2):
                    kc = 2 * p + j
                    S.op("pe", lambda e: e.matmul(psS.ap(0, 128, j * 512, (j + 1) * 512), k.h[0:96, kc * 128:(kc + 1) * 128],
                                                  q.h[0:96, qt * 512:(qt + 1) * 512], start=True, stop=True),
                         reads=[k.b, q.b], writes=psS.bufs)
                pm = Pm.next()
                S.op("act", lambda e: e.activation(pm.h[:, :], psS.ap(0, 128, 0, 1024), AF.Exp, scale=float(MLA_SCALE)),
                     reads=psS.bufs, writes=[pm.b])
                state[idx] = pm

            s_mm(0)
            s_mm(1)
            for idx in range(len(items)):
                qt, p = items[idx]
                if p == 0:
                    psO_of[qt] = psO_r.next()
                psO = psO_of[qt]
                if idx + 2 < len(items):
                    s_mm(idx + 2)
                pm = state.pop(idx)
                for j in range(2):
                    kc = 2 * p + j
                    S.op("pe", lambda e: e.matmul(psO.ap(0, 65, 0, 512), v.h[:, kc, 0:65], pm.h[:, j * 512:(j + 1) * 512],
                                                  start=(kc == 0), stop=(kc == 31)),
                         reads=[v.b, pm.b], writes=psO.bufs)
                if p == 10:
                    flush()
                if p == 15:
                    normalize(psO, 1, h, qt * 512)

        loads(0)
        loads(1)
        w3 = load_w3(l)
        for h in range(H):
            if 1 <= h and h + 1 < H:
                loads(h + 1)
            na_head(h)
            mla_head(h)
        flush()
        barrier()
        return w3

    def phase3(l, w3):
        wg, wbn, wbm, wo = w3
        x_src = xT_in if l == 0 else xT_d
        w1 = None
        WK = Arena(6 * 1024, 78 * 1024)
        xt = WK.alloc("p3xt", [128, 8, TT], F32)
        hTs = Ring([WK.alloc("p3h%d" % i, [128, 8, TT], BF16) for i in range(2)])
        onas = Ring([WK.alloc("p3ona%d" % i, [128, 4, TT], BF16) for i in range(2)])
        omls = Ring([WK.alloc("p3oml%d" % i, [128, 4, TT], BF16) for i in range(2)])
        mg = WK.alloc("p3mg", [128, 8, TT], BF16)
        gring = Ring([WK.alloc("p3g%d" % i, [128, TT], F32) for i in range(4)])
        tring = Ring([WK.alloc("p3t%d" % i, [128, TT], F32) for i in range(4)])
        gate_col = l * 48 + 16
        cur = {}

        def loads(t):
            a0, a1 = t * TT, (t + 1) * TT
            hT, ona, oml = hTs.next(), onas.next(), omls.next()
            load(hT.b.name, hT.h[:, :, :], hT_d[:, a0:a1].rearrange("(k p) t -> p k t", p=128), [hT.b])
            load(ona.b.name, ona.h[:, :, :], oT_d[0, :, a0:a1].rearrange("(k p) t -> p k t", p=128), [ona.b])
            load(oml.b.name, oml.h[:, :, :], oT_d[1, :, a0:a1].rearrange("(k p) t -> p k t", p=128), [oml.b])
            cur[t] = (hT, ona, oml)

        loads(0)
        for t in range(NT):
            c0, c1 = t * TT, (t + 1) * TT
            hT, ona, oml = cur.pop(t)
            load("p3xt", xt.h[:, :, :], x_src[:, c0:c1].rearrange("(k p) t -> p k t", p=128), [xt.b])
            if t + 1 < NT:
                loads(t + 1)
            if t == 1:
                w1 = load_w1ff(l)
            for c in range(8):
                gts = []
                for br in range(2):
                    ps = ps_ring7.next()
                    colw = br * 1024 + c * 128
                    for k in range(8):
                        S.op("pe", lambda e: e.matmul(ps.ap(0, 128, 0, TT), wg.h[:, k, colw:colw + 128], hT.h[:, k, :],
                                                      start=(k == 0), stop=(k == 7)), reads=[wg.b, hT.b], writes=ps.bufs)
                    g = gring.next()
                    bcol = l * 16 + br * 8 + c
                    S.op("act", lambda e: e.activation(g.h[:, :], ps.ap(0, 128, 0, TT), AF.Sigmoid,
                                                       bias=bgate.h[:, bcol:bcol + 1]), reads=ps.bufs + [bgate.b], writes=[g.b])
                    gts.append(g)
                tts = []
                for br, (w, o) in enumerate(((wbn, ona), (wbm, oml))):
                    ps = ps_ring7.next()
                    for k in range(4):
                        S.op("pe", lambda e: e.matmul(ps.ap(0, 128, 0, TT), w.h[:, k, c * 128:(c + 1) * 128], o.h[:, k, :],
                                                      start=(k == 0), stop=(k == 3)), reads=[w.b, o.b], writes=ps.bufs)
                    tt = tring.next()
                    S.op("dve", lambda e: e.tensor_tensor(tt.h[:, :], ps.ap(0, 128, 0, TT), gts[br].h[:, :], ALU.mult),
                         reads=ps.bufs + [gts[br].b], writes=[tt.b])
                    tts.append(tt)
                S.op("dve", lambda e: e.tensor_tensor(mg.h[:, c, :], tts[0].h[:, :], tts[1].h[:, :], ALU.add),
                     reads=[tts[0].b, tts[1].b], writes=[mg.b])
            def mm_out(c2, ps):
                for k in range(8):
                    S.op("pe", lambda e: e.matmul(ps.ap(0, 128, 0, TT), wo.h[:, k, c2 * 128:(c2 + 1) * 128], mg.h[:, k, :],
                                                  start=(k == 0), stop=(k == 7)), reads=[wo.b, mg.b], writes=ps.bufs)
            resid_norm_loop(xt, hT, gate_col, mm_out, ps_ring7)
            store("st_p3xt", xT_d[:, c0:c1].rearrange("(k p) t -> p k t", p=128), xt.h[:, :, :], [xt.b])
            rstd_t = gring.next()
            rstd_finish(ps_ss, D, rstd_t)
            for k in range(8):
                tmp = tring.next()
                S.op("dve", lambda e: e.tensor_tensor(tmp.h[:, :], xt.h[:, k, :], rstd_t.h[:, :], ALU.mult),
                     reads=[xt.b, rstd_t.b], writes=[tmp.b])
                S.op("act", lambda e: e.activation(hT.h[:, k, :], tmp.h[:, :], AF.Identity,
                                                   bias=modv.h[:, l * 48 + 24 + k:l * 48 + 25 + k],
                                                   scale=gsM.h[:, l * 8 + k:l * 8 + k + 1]),
                     reads=[tmp.b, modv.b, gsM.b], writes=[hT.b])
            store("st_" + hT.b.name, hT_d[:, c0:c1].rearrange("(k p) t -> p k t", p=128), hT.h[:, :, :], [hT.b])
        barrier()
        return w1

    def phase4(l, last, w1):
        A = Arena(142 * 1024, SB_LIMIT)
        w2 = A.alloc("w2", [128, 32, 1024], BF16)
        for f in range(0, 32, 4):
            S.dma("pool", "w2", w2.h[:, f:f + 4, :], w_ff2[l, f * 128:(f + 4) * 128, :].rearrange("(f p) c -> p f c", p=128),
                  reads=[], writes=[w2.b])
        WK = Arena(6 * 1024, 78 * 1024)
        xt = WK.alloc("p4xt", [128, 8, TT], F32)
        h2s = Ring([WK.alloc("p4h%d" % i, [128, 8, TT], BF16) for i in range(2)])
        aT = WK.alloc("p4a", [128, 32, TT], BF16)
        rstd_t = WK.alloc("p4rstd", [128, TT], F32)
        tmp_ring = Ring([WK.alloc("p4tmp%d" % i, [128, TT], F32) for i in range(3)])
        gate_col = l * 48 + 40
        cur = {}

        def loads(t):
            a0, a1 = t * TT, (t + 1) * TT
            h2 = h2s.next()
            load(h2.b.name, h2.h[:, :, :], hT_d[:, a0:a1].rearrange("(k p) t -> p k t", p=128), [h2.b])
            cur[t] = h2

        loads(0)
        for t in range(NT):
            c0, c1 = t * TT, (t + 1) * TT
            h2 = cur.pop(t)
            load("p4xt", xt.h[:, :, :], xT_d[:, c0:c1].rearrange("(k p) t -> p k t", p=128), [xt.b])
            if t + 1 < NT:
                loads(t + 1)
            for f in range(32):
                ps = ps_ring7.next()
                for k in range(8):
                    S.op("pe", lambda e: e.matmul(ps.ap(0, 128, 0, TT), w1.h[:, k, f * 128:(f + 1) * 128], h2.h[:, k, :],
                                                  start=(k == 0), stop=(k == 7)), reads=[w1.b, h2.b], writes=ps.bufs)
                tmp = tmp_ring.next()
                S.op("act", lambda e: e.activation(tmp.h[:, :], ps.ap(0, 128, 0, TT), AF.Relu), reads=ps.bufs, writes=[tmp.b])
                S.op("dve", lambda e: e.tensor_tensor(aT.h[:, f, :], tmp.h[:, :], tmp.h[:, :], ALU.mult),
                     reads=[tmp.b], writes=[aT.b])
            def mm_ff2(c2, ps):
                for f in range(32):
                    S.op("pe", lambda e: e.matmul(ps.ap(0, 128, 0, TT), w2.h[:, f, c2 * 128:(c2 + 1) * 128], aT.h[:, f, :],
                                                  start=(f == 0), stop=(f == 31)), reads=[w2.b, aT.b], writes=ps.bufs)
            resid_norm_loop(xt, h2, gate_col, mm_ff2, ps_ring7)
            rstd_finish(ps_ss, D, rstd_t)
            if not last:
                store("st_p4xt", xT_d[:, c0:c1].rearrange("(k p) t -> p k t", p=128), xt.h[:, :, :], [xt.b])
                for k in range(8):
                    tmp = tmp_ring.next()
                    S.op("dve", lambda e: e.tensor_tensor(tmp.h[:, :], xt.h[:, k, :], rstd_t.h[:, :], ALU.mult),
                         reads=[xt.b, rstd_t.b], writes=[tmp.b])
                    S.op("act", lambda e: e.activation(h2.h[:, k, :], tmp.h[:, :], AF.Identity,
                                                       bias=modv.h[:, (l + 1) * 48 + k:(l + 1) * 48 + k + 1],
                                                       scale=gsA.h[:, (l + 1) * 8 + k:(l + 1) * 8 + k + 1]),
                         reads=[tmp.b, modv.b, gsA.b], writes=[h2.b])
                store("st_" + h2.b.name, hT_d[:, c0:c1].rearrange("(k p) t -> p k t", p=128), h2.h[:, :, :], [h2.b])
            else:
                for k in range(8):
                    tmp = tmp_ring.next()
                    S.op("dve", lambda e: e.tensor_tensor(tmp.h[:, :], xt.h[:, k, :], rstd_t.h[:, :], ALU.mult),
                         reads=[xt.b, rstd_t.b], writes=[tmp.b])
                    S.op("act", lambda e: e.activation(xt.h[:, k, :], tmp.h[:, :], AF.Copy, scale=gfin.h[:, k:k + 1]),
                         reads=[tmp.b, gfin.b], writes=[xt.b])
                store("st_out", outT[:, c0:c1].rearrange("(k p) t -> p k t", p=128), xt.h[:, :, :], [xt.b])
        barrier()

    phase0()
    for l in range(Ln):
        phase1(l)
        if stop_after == (l, "p1"):
            break
        w3 = phase2(l)
        if stop_after == (l, "p2"):
            break
        w1 = phase3(l, w3)
        if stop_after == (l, "p3"):
            break
        phase4(l, l == Ln - 1, w1)
    barrier()
    S.emit()
    return nc, S


_CACHE = {}


def _colsT(v, n):
    return np.ascontiguousarray(np.swapaxes(v.reshape(v.shape[:-1] + (n, 128)), -1, -2))


def _colsTL(v, n):
    t = _colsT(v, n)
    return np.ascontiguousarray(t.transpose(1, 0, 2).reshape(128, -1))


def prep_shared(inp):
    L = L_DEPTH
    w_in = inp["w_in"]
    kr = w_in[:, :, 1920:1952]
    kr_sw = np.concatenate([kr[:, :, 16:32], kr[:, :, 0:16]], axis=-1)
    w_inA = np.ascontiguousarray(np.concatenate([w_in[:, :, :1952], kr_sw], axis=-1))
    w_gate = np.ascontiguousarray(w_in[:, :, 1952:])
    wuq = inp["w_uq"].reshape(L, 256, H, 96)
    nope = wuq[:, :, :, :64].reshape(L, 256, 512)
    rope = wuq[:, :, :, 64:]
    rope_sw = np.concatenate([rope[..., 16:32], rope[..., 0:16]], axis=-1)
    w_uqp = np.ascontiguousarray(np.concatenate([nope, rope.reshape(L, 256, 256), rope_sw.reshape(L, 256, 256)], axis=-1))
    wukv = inp["w_ukv"].reshape(L, 128, H, 128)
    w_uk = np.ascontiguousarray(wukv[:, :, :, :64].reshape(L, 128, 512))
    w_uv = np.ascontiguousarray(wukv[:, :, :, 64:].reshape(L, 128, 512))
    idx_r, idx_c, msk = build_na_index()
    rpb = inp["rpb"]
    nab = np.where(msk[None, None], rpb[:, :, idx_r, idx_c], np.float32(MASK_VAL)).astype(np.float32)
    nab = np.ascontiguousarray(nab.transpose(0, 1, 3, 2, 4).reshape(L, H, 128, NAB_W))
    p = np.arange(128)
    i32 = p % 32
    invf = (1.0 / (10000.0 ** (np.arange(0, 32, 2, dtype=np.float32) / 32.0))).astype(np.float32)
    cst = np.zeros((128, 4), np.float32)
    cst[:, 0] = invf[i32 % 16]
    cst[:, 1] = np.where(i32 < 16, -1.0, 1.0)
    return dict(
        cst=cst, w_ada=inp["w_ada"], b_adaT=_colsTL(inp["b_ada"], 48), g_mixT=_colsTL(inp["g_mix"], 8),
        g_mlpT=_colsTL(inp["g_mlp"], 8), g_finT=_colsT(inp["g_final"], 8), w_inA=w_inA, w_gate=w_gate,
        b_gateT=_colsTL(inp["b_gate"], 16), g_qT=_colsTL(inp["g_q"], 2), g_kvT=_colsTL(inp["g_kv"], 1),
        w_uqp=w_uqp, w_uk=w_uk, w_uv=w_uv, w_br_na=inp["w_br_na"], w_br_mla=inp["w_br_mla"], w_out=inp["w_out"],
        w_ff1=inp["w_ff1"], w_ff2=inp["w_ff2"], nab=nab)


def make_in_maps(inp, cores):
    inp = {k: np.asarray(v) for k, v in inp.items()}
    shared = prep_shared(inp)
    maps = []
    for b in cores:
        m = dict(shared)
        m["xT"] = np.ascontiguousarray(inp["x"][b].T)
        m["cT"] = _colsT(inp["c"][b], 8)
        m["pos"] = np.ascontiguousarray(inp["positions"][b].reshape(1, S_TOK).astype(np.int32))
        maps.append(m)
    return maps


def kernel(**inputs):
    if "nc" not in _CACHE:
        _CACHE["nc"] = build_program()[0]
    nc = _CACHE["nc"]
    in_maps = make_in_maps(inputs, list(range(8)))
    res = run_bass_kernel_spmd(nc, in_maps, core_ids=list(range(8)))
    out = np.stack([np.ascontiguousarray(r["outT"].T) for r in res.results], axis=0)
    return out.astype(np.float32)
```

```python
import numpy as np
import concourse.bass as bass
import concourse.mybir as mybir
from concourse.bass_utils import run_bass_kernel_spmd

F32 = mybir.dt.float32
BF16 = mybir.dt.bfloat16
I32 = mybir.dt.int32
ALU = mybir.AluOpType
AF = mybir.ActivationFunctionType

ENGS = ("pe", "act", "dve", "pool", "sp")


class Buf:
    __slots__ = ("name", "w", "r", "excl")

    def __init__(self, name, excl=False):
        self.name = name
        self.w = {}
        self.r = {}
        self.excl = excl


class _Rec:
    def __getattr__(self, name):
        def f(*a, **k):
            self.call = (name, a, k)
            return self
        return f


class Sched:
    def __init__(self, nc):
        self.nc = nc
        self.q = {e: [] for e in ENGS}
        self.cnt = {e: 0 for e in ENGS}
        self.sem = {}
        for e in ("pe", "act", "dve", "pool"):
            self.sem[e] = nc.alloc_semaphore("prog_" + e)
        self.known = {e: {} for e in ENGS}
        self.hist = {e: {} for e in ENGS}
        self.semeng = {id(self.sem[e]): e for e in self.sem}
        self.dsem = {}
        self.nwait = 0

    def _need(self, eng, ev):
        sem, val = ev
        k = self.known[eng]
        if k.get(id(sem), 0) >= val:
            return
        self.q[eng].append(("w", sem, val))
        self.nwait += 1
        k[id(sem)] = val
        src = self.semeng.get(id(sem))
        if src is not None:
            snap = self.hist[src].get(val)
            if snap:
                for s, v in snap.items():
                    if k.get(s, 0) < v:
                        k[s] = v

    def _deps(self, eng, reads, writes):
        own = id(self.sem[eng]) if eng in self.sem else None
        for b in reads:
            for sid, ev in b.w.items():
                self._need(eng, ev)
            if b.excl:
                for sid, ev in b.r.items():
                    if sid != own:
                        self._need(eng, ev)
        for b in writes:
            for sid, ev in b.w.items():
                if sid == own:
                    continue
                self._need(eng, ev)
            for sid, ev in b.r.items():
                if sid == own:
                    continue
                self._need(eng, ev)

    def _mark(self, ev, reads, writes):
        sem, val = ev
        for b in writes:
            b.w = {id(sem): ev}
            b.r = {}
        for b in reads:
            b.r[id(sem)] = ev

    def op(self, eng, fn, reads=(), writes=()):
        rec = _Rec()
        fn(rec)
        call = rec.call
        fn = lambda e, call=call: getattr(e, call[0])(*call[1], **call[2])
        self._deps(eng, reads, writes)
        self.cnt[eng] += 1
        seq = self.cnt[eng]
        sem = self.sem[eng]
        self.q[eng].append(("o", fn, sem, 1))
        snap = dict(self.known[eng])
        snap[id(sem)] = seq - 1
        self.hist[eng][seq] = snap
        self._mark((sem, seq), reads, writes)

    def dma(self, eng, key, out, in_, reads=(), writes=(), transpose=False):
        self._deps(eng, reads, writes)
        if key not in self.dsem:
            self.dsem[key] = [self.nc.alloc_semaphore("d_" + key), 0]
        ent = self.dsem[key]
        ent[1] += 16
        if transpose:
            fn = lambda e, o=out, i=in_: e.dma_start_transpose(out=o, in_=i)
        else:
            fn = lambda e, o=out, i=in_: e.dma_start(out=o, in_=i)
        self.q[eng].append(("o", fn, ent[0], 16))
        self._mark((ent[0], ent[1]), reads, writes)

    def wait_all(self, eng, bufs):
        for b in bufs:
            for ev in list(b.w.values()):
                self._need(eng, ev)

    def emit(self):
        nc = self.nc
        emap = {"pe": "tensor", "act": "scalar", "dve": "vector", "pool": "gpsimd", "sp": "sync"}
        with nc.Block() as block:
            for e in ENGS:
                items = self.q[e]

                def body(engine, items=items):
                    for it in items:
                        if it[0] == "w":
                            engine.wait_ge(it[1], it[2])
                        else:
                            it[1](engine).then_inc(it[2], it[3])

                getattr(block, emap[e])(body)


L_DEPTH = 4
D = 1024
S_TOK = 4096
TT = 512
NT = S_TOK // TT
DC = D // 128
H = 8
NA_SCALE = 64 ** -0.5
MLA_SCALE = 96 ** -0.5
EPS = 1e-6
NPAT = 7
NAB_W = NPAT * 640
MASK_VAL = -200.0
TWO_PI = 2.0 * np.pi
C1 = 6.28125
C2 = TWO_PI - 6.28125
MAGIC = 12582912.0


def na_chunks(i):
    j0 = min(max(i - 2, 0), 27)
    return [j0 + s for s in range(5)]


def na_pat(i):
    if i <= 1:
        return i
    if i >= 30:
        return i - 25
    return 2 + min(max(i - 2, 0), 2) if False else (2 if i < 30 else 0)


def na_pattern_index(i):
    if i == 0:
        return 0
    if i == 1:
        return 1
    if i == 30:
        return 3
    if i == 31:
        return 4
    return 2


NA_REP = [0, 1, 10, 30, 31, 10, 10]


def build_na_index():
    idx_r = np.zeros((NPAT, 128, 640), np.int64)
    idx_c = np.zeros((NPAT, 128, 640), np.int64)
    msk = np.zeros((NPAT, 128, 640), bool)
    rows = 64
    for p in range(NPAT):
        i = NA_REP[p]
        J = na_chunks(i)
        for s in range(5):
            for a in range(2):
                kr = 2 * J[s] + a
                for b in range(2):
                    r = 2 * i + b
                    rs = min(max(r - 4, 0), rows - 8)
                    vrow = (rs <= kr < rs + 8)
                    for kc in range(64):
                        qc = np.arange(64)
                        cs = np.clip(qc - 8, 0, 64 - 16)
                        v = (cs <= kc) & (kc < cs + 16) & vrow
                        dr = kr - r + 7
                        dc = kc - qc + 15
                        key = a * 64 + kc
                        cols = s * 128 + b * 64 + qc
                        msk[p, key, cols] = v
                        idx_r[p, key, cols] = np.clip(dr, 0, 14)
                        idx_c[p, key, cols] = np.clip(dc, 0, 30)
    return idx_r, idx_c, msk


class T:
    __slots__ = ("h", "b")

    def __init__(self, h, name):
        self.h = h
        self.b = Buf(name)


class Ring:
    def __init__(self, items):
        self.items = items
        self.i = 0

    def next(self):
        it = self.items[self.i % len(self.items)]
        self.i += 1
        return it


def build_program(n_layers=L_DEPTH, debug=False, stop_after=None):
    nc = bass.Bass("TRN2", target_bir_lowering=False)
    S = Sched(nc)
    Ln = n_layers

    def din(name, shape, dt=F32):
        return nc.dram_tensor(name, list(shape), dt, kind="ExternalInput").ap()

    def dscr(name, shape, dt):
        kind = "ExternalOutput" if debug else "Internal"
        return nc.dram_tensor(name, list(shape), dt, kind=kind).ap()

    xT_in = din("xT", [D, S_TOK])
    cT_in = din("cT", [128, 8])
    pos_in = din("pos", [1, S_TOK], I32)
    cst_in = din("cst", [128, 4])
    w_ada = din("w_ada", [L_DEPTH, D, 6 * D])
    b_adaT = din("b_adaT", [128, L_DEPTH * 48])
    g_mixT = din("g_mixT", [128, L_DEPTH * 8])
    g_mlpT = din("g_mlpT", [128, L_DEPTH * 8])
    g_finT = din("g_finT", [128, 8])
    w_inA = din("w_inA", [L_DEPTH, D, 1984])
    w_gate = din("w_gate", [L_DEPTH, D, 2048])
    b_gateT = din("b_gateT", [128, L_DEPTH * 16])
    g_qT = din("g_qT", [128, L_DEPTH * 2])
    g_kvT = din("g_kvT", [128, L_DEPTH * 1])
    w_uqp = din("w_uqp", [L_DEPTH, 256, 1024])
    w_uk = din("w_uk", [L_DEPTH, 128, 512])
    w_uv = din("w_uv", [L_DEPTH, 128, 512])
    w_brna = din("w_br_na", [L_DEPTH, 512, D])
    w_brml = din("w_br_mla", [L_DEPTH, 512, D])
    w_out = din("w_out", [L_DEPTH, D, D])
    w_ff1 = din("w_ff1", [L_DEPTH, D, 4 * D])
    w_ff2 = din("w_ff2", [L_DEPTH, 4 * D, D])
    nab_in = din("nab", [L_DEPTH, H, 128, NAB_W])
    outT = nc.dram_tensor("outT", [D, S_TOK], F32, kind="ExternalOutput").ap()

    xT_d = dscr("xT_d", [D, S_TOK], F32)
    hT_d = dscr("hT_d", [D, S_TOK], BF16)
    qna_d = dscr("qna_d", [512, S_TOK], BF16)
    kna_d = dscr("kna_d", [512, S_TOK], BF16)
    vna_d = dscr("vna_d", [128, 32, H, 65], BF16)
    qm_d = dscr("qm_d", [H, 96, S_TOK], BF16)
    km_d = dscr("km_d", [H, 64, S_TOK], BF16)
    kr_d = dscr("kr_d", [32, S_TOK], BF16)
    vm_d = dscr("vm_d", [128, 32, H, 65], BF16)
    oT_d = dscr("oT_d", [2, 512, S_TOK], BF16)
    cos_d = dscr("cos_d", [128, S_TOK], F32)
    sin_d = dscr("sin_d", [128, S_TOK], F32)
    recd_d = dscr("recd_d", [4, 512], F32)
    dram_b = Buf("dram")

    SB_BASE = 16512
    SB_LIMIT = 206 * 1024
    arena = {"off": 0}

    def sb(name, shape, dt, off=None):
        esz = 4 if dt in (F32, I32) else 2
        nbytes = int(np.prod(shape[1:])) * esz
        nbytes = (nbytes + 31) // 32 * 32
        if off is None:
            off = arena["off"]
            arena["off"] = off + nbytes
        assert off + nbytes <= SB_LIMIT, (name, off, nbytes)
        h = nc.alloc_sbuf_tensor_at(name, list(shape), dt, offset=SB_BASE + off)
        return T(h, name)

    class Arena:
        def __init__(self, base, limit):
            self.base = base
            self.off = base
            self.limit = limit

        def reset(self):
            self.off = self.base

        def alloc(self, name, shape, dt):
            esz = 4 if dt in (F32, I32) else 2
            nbytes = int(np.prod(shape[1:])) * esz
            nbytes = (nbytes + 31) // 32 * 32
            assert self.off + nbytes <= self.limit, (name, self.off, nbytes, self.limit)
            t = sb(name, shape, dt, off=self.off)
            self.off += nbytes
            return t

    CONST = Arena(0, 6 * 1024)
    ones_bf = CONST.alloc("ones_bf", [128, 128], BF16)
    ones_f = CONST.alloc("ones_f", [128, 64], F32)
    modv = CONST.alloc("modv", [128, L_DEPTH * 48], F32)
    gsA = CONST.alloc("gsA", [128, L_DEPTH * 8], F32)
    gsM = CONST.alloc("gsM", [128, L_DEPTH * 8], F32)
    gmix = CONST.alloc("gmix", [128, L_DEPTH * 8], F32)
    gmlp = CONST.alloc("gmlp", [128, L_DEPTH * 8], F32)
    gfin = CONST.alloc("gfin", [128, 8], F32)
    bgate = CONST.alloc("bgate", [128, L_DEPTH * 16], F32)
    gq = CONST.alloc("gq", [128, L_DEPTH * 2], F32)
    gkv = CONST.alloc("gkv", [128, L_DEPTH], F32)
    cst = CONST.alloc("cst", [128, 4], F32)
    cact = CONST.alloc("cact", [128, 8], F32)
    eps_t = CONST.alloc("eps_t", [128, 8], F32)
    WREG = Arena(6 * 1024, 134 * 1024)
    WORK = Arena(134 * 1024, SB_LIMIT)

    ps_all = nc.alloc_psum_tensor("ps_all", [128, 4096], F32)
    bankb = [Buf("bank%d" % i, excl=True) for i in range(8)]

    class PS:
        def __init__(self, b0, nb=1):
            self.b0 = b0
            self.nb = nb
            self.bufs = [bankb[b0 + i] for i in range(nb)]

        def ap(self, p0, p1, c0, c1):
            return ps_all[p0:p1, self.b0 * 512 + c0:self.b0 * 512 + c1]

    ps_ring = Ring([PS(i) for i in range(8)])

    uid = {"n": 0}

    def barrier():
        evs = []
        for e in ("pe", "act", "dve", "pool"):
            if S.cnt[e] > 0:
                evs.append((S.sem[e], S.cnt[e]))
        for k, ent in S.dsem.items():
            if ent[1] > 0:
                evs.append((ent[0], ent[1]))
        for e in ENGS:
            for ev in evs:
                S._need(e, ev)

    def store(key, out, in_, reads, eng="sp"):
        S.dma(eng, key, out, in_, reads=reads, writes=[])

    def load(key, out, in_, writes, eng="sp"):
        S.dma(eng, key, out, in_, reads=[], writes=writes)

    def phase0():
        WREG.reset()
        WORK.reset()
        S.op("dve", lambda e: e.memset(ones_bf.h[:, :], 1.0), writes=[ones_bf.b])
        S.op("dve", lambda e: e.memset(ones_f.h[:, :], 1.0), writes=[ones_f.b])
        S.op("dve", lambda e: e.memset(eps_t.h[:, :], float(EPS)), writes=[eps_t.b])
        load("cst", cst.h[:, :], cst_in, [cst.b])
        load("cact", cact.h[:, :], cT_in, [cact.b])
        load("gmix", gmix.h[:, :], g_mixT, [gmix.b])
        load("gmlp", gmlp.h[:, :], g_mlpT, [gmlp.b])
        load("gfin", gfin.h[:, :], g_finT, [gfin.b])
        load("bgate", bgate.h[:, :], b_gateT, [bgate.b])
        load("gq", gq.h[:, :], g_qT, [gq.b])
        load("gkv", gkv.h[:, :], g_kvT, [gkv.b])
        S.op("act", lambda e: e.activation(cact.h[:, :], cact.h[:, :], AF.Silu), reads=[cact.b], writes=[cact.b])
        import os
        P0SKIP = os.environ.get("P0SKIP", "")
        posi = WORK.alloc("posi", [128, S_TOK], I32)
        ang = WORK.alloc("ang", [128, S_TOK], F32)
        t1 = WORK.alloc("rt1", [128, S_TOK], F32)
        t2 = WORK.alloc("rt2", [128, S_TOK], F32)
        load("posi", posi.h[:, :], pos_in.partition_broadcast(128), [posi.b])
        S.op("dve", lambda e: e.tensor_copy(ang.h[:, :], posi.h[:, :]), reads=[posi.b], writes=[ang.b])
        S.op("dve", lambda e: e.tensor_scalar(ang.h[:, :], ang.h[:, :], cst.h[:, 0:1], None, ALU.mult),
             reads=[ang.b, cst.b], writes=[ang.b])
        for which, dst in ((0, sin_d), (1, cos_d)):
            if "rope" in P0SKIP:
                break
            if which == 1:
                S.op("dve", lambda e: e.tensor_scalar(ang.h[:, :], ang.h[:, :], float(np.pi / 2), None, ALU.add),
                     reads=[ang.b], writes=[ang.b])
            S.op("dve", lambda e: e.tensor_scalar(t1.h[:, :], ang.h[:, :], float(1.0 / TWO_PI), MAGIC, ALU.mult, ALU.add),
                 reads=[ang.b], writes=[t1.b])
            S.op("dve", lambda e: e.tensor_scalar(t1.h[:, :], t1.h[:, :], -MAGIC, None, ALU.add),
                 reads=[t1.b], writes=[t1.b])
            S.op("dve", lambda e: e.scalar_tensor_tensor(t2.h[:, :], t1.h[:, :], -C1, ang.h[:, :], ALU.mult, ALU.add),
                 reads=[t1.b, ang.b], writes=[t2.b])
            S.op("dve", lambda e: e.scalar_tensor_tensor(t2.h[:, :], t1.h[:, :], -float(C2), t2.h[:, :], ALU.mult, ALU.add),
                 reads=[t1.b, t2.b], writes=[t2.b])
            S.op("dve", lambda e: e.tensor_scalar(t2.h[:, :], t2.h[:, :], -3.1415925, 3.1415925, ALU.max, ALU.min),
                 reads=[t2.b], writes=[t2.b])
            if which == 0:
                S.op("act", lambda e: e.activation(t2.h[:, :], t2.h[:, :], AF.Sin, scale=cst.h[:, 1:2]),
                     reads=[t2.b, cst.b], writes=[t2.b])
            else:
                S.op("act", lambda e: e.activation(t2.h[:, :], t2.h[:, :], AF.Sin), reads=[t2.b], writes=[t2.b])
            store("st_rope", dst, t2.h[:, :], [t2.b])
        stg = [WREG.alloc("adastg%d" % i, [128, 8, 1024], F32) for i in range(2)]
        badd = WREG.alloc("badd", [128, L_DEPTH * 48], F32)
        modrow = WREG.alloc("modrow", [128, 6 * D], F32)
        load("badd", badd.h[:, :], b_adaT, [badd.b])
        n = 0
        for l in range(Ln):
            if "mod" in P0SKIP:
                break
            for g in range(6):
                st = stg[n % 2]
                n += 1
                load(st.b.name, st.h[:, :, :],
                     w_ada[l, :, g * 1024:(g + 1) * 1024].rearrange("(k p) c -> p k c", p=128), [st.b])
                for hf in range(2):
                    ps = ps_ring.next()
                    for k in range(8):
                        S.op("pe", lambda e: e.matmul(ps.ap(0, 1, 0, 512), cact.h[:, k:k + 1], st.h[:, k, hf * 512:(hf + 1) * 512],
                                                      start=(k == 0), stop=(k == 7)), reads=[st.b, cact.b], writes=ps.bufs)
                    c0 = g * 1024 + hf * 512
                    S.op("act", lambda e: e.activation(modrow.h[0:1, c0:c0 + 512], ps.ap(0, 1, 0, 512), AF.Copy),
                         reads=ps.bufs, writes=[modrow.b])
            psm = ps_ring.next()
            for j in range(48):
                S.op("pe", lambda e: e.matmul(psm.ap(0, 128, j, j + 1), modrow.h[0:1, j * 128:(j + 1) * 128], ones_f.h[0:1, 0:1],
                                              start=True, stop=True), reads=[modrow.b, ones_f.b], writes=psm.bufs)
            S.op("dve", lambda e, l=l, psm=psm: e.tensor_tensor(
                modv.h[:, l * 48:(l + 1) * 48], psm.ap(0, 128, 0, 48), badd.h[:, l * 48:(l + 1) * 48], ALU.add),
                reads=psm.bufs + [badd.b], writes=[modv.b])
            S.op("dve", lambda e, l=l: e.scalar_tensor_tensor(
                gsA.h[:, l * 8:(l + 1) * 8], modv.h[:, l * 48 + 8:l * 48 + 16], 1.0, gmix.h[:, l * 8:(l + 1) * 8],
                ALU.add, ALU.mult), reads=[modv.b, gmix.b], writes=[gsA.b])
            S.op("dve", lambda e, l=l: e.scalar_tensor_tensor(
                gsM.h[:, l * 8:(l + 1) * 8], modv.h[:, l * 48 + 32:l * 48 + 40], 1.0, gmlp.h[:, l * 8:(l + 1) * 8],
                ALU.add, ALU.mult), reads=[modv.b, gmlp.b], writes=[gsM.b])
        barrier()

    def rstd_from_sq(sq_aps, sq_bufs, dim, rstd_t):
        pss = ps_ring.next()
        n = len(sq_aps)
        for k in range(n):
            S.op("pe", lambda e, k=k, pss=pss: e.matmul(pss.ap(0, 128, 0, TT), ones_bf.h[:, :], sq_aps[k],
                                                        start=(k == 0), stop=(k == n - 1)),
                 reads=[ones_bf.b] + sq_bufs, writes=pss.bufs)
        S.op("act", lambda e, pss=pss: e.activation(rstd_t.h[:, :], pss.ap(0, 128, 0, TT), AF.Ln,
                                                    bias=eps_t.h[:, 0:1], scale=float(1.0 / dim)),
             reads=pss.bufs + [eps_t.b], writes=[rstd_t.b])
        S.op("act", lambda e: e.activation(rstd_t.h[:, :], rstd_t.h[:, :], AF.Exp, scale=-0.5),
             reads=[rstd_t.b], writes=[rstd_t.b])

    ps_ring7 = Ring([PS(i) for i in range(7)])
    ps_ss = PS(7)

    def rstd_finish(pss, dim, rstd_t):
        S.op("act", lambda e: e.activation(rstd_t.h[:, :], pss.ap(0, 128, 0, TT), AF.Ln,
                                           bias=eps_t.h[:, 0:1], scale=float(1.0 / dim)),
             reads=pss.bufs + [eps_t.b], writes=[rstd_t.b])
        S.op("act", lambda e: e.activation(rstd_t.h[:, :], rstd_t.h[:, :], AF.Exp, scale=-0.5),
             reads=[rstd_t.b], writes=[rstd_t.b])

    def resid_norm_loop(xt, hbuf, gate_col, mm_group, ring):
        def ss_mm(k):
            S.op("pe", lambda e: e.matmul(ps_ss.ap(0, 128, 0, TT), ones_bf.h[:, :], hbuf.h[:, k, :],
                                          start=(k == 0), stop=(k == 7)), reads=[ones_bf.b, hbuf.b], writes=ps_ss.bufs)
        for c2 in range(8):
            ps = ring.next()
            mm_group(c2, ps)
            if c2 >= 1:
                ss_mm(c2 - 1)
            S.op("dve", lambda e: e.scalar_tensor_tensor(xt.h[:, c2, :], ps.ap(0, 128, 0, TT),
                                                         modv.h[:, gate_col + c2:gate_col + c2 + 1], xt.h[:, c2, :],
                                                         ALU.mult, ALU.add), reads=ps.bufs + [modv.b, xt.b], writes=[xt.b])
            S.op("act", lambda e: e.activation(hbuf.h[:, c2, :], xt.h[:, c2, :], AF.Square), reads=[xt.b], writes=[hbuf.b])
        ss_mm(7)

    def norm_tile(xt, hout, hout_b, gs_t, gcol, shcol, sq_all, sq_k, sq_b, rstd_t, tmp_ring):
        S.op("act", lambda e: e.activation(sq_all, xt.h[:, :, :], AF.Square), reads=[xt.b], writes=[sq_b])
        rstd_from_sq([sq_k(k) for k in range(8)], [sq_b], D, rstd_t)
        for k in range(8):
            tmp = tmp_ring.next()
            S.op("dve", lambda e: e.tensor_tensor(tmp.h[:, :], xt.h[:, k, :], rstd_t.h[:, :], ALU.mult),
                 reads=[xt.b, rstd_t.b], writes=[tmp.b])
            S.op("act", lambda e: e.activation(
                hout(k), tmp.h[:, :], AF.Identity, bias=modv.h[:, shcol + k:shcol + k + 1],
                scale=gs_t.h[:, gcol + k:gcol + k + 1]), reads=[tmp.b, modv.b, gs_t.b], writes=[hout_b])

    def load_weight_bf16(dst_t, dst_ap, src_ap):
        S.dma("pool", dst_t.b.name, dst_ap, src_ap, reads=[], writes=[dst_t.b])

    def phase1(l):
        WREG.reset()
        WORK.reset()
        x_src = xT_in if l == 0 else xT_d
        wA = WREG.alloc("wA", [128, 8, 1984], BF16)
        wq = WREG.alloc("wq", [128, 2, 1024], BF16)
        wk = WREG.alloc("wk", [128, 512], BF16)
        wv = WREG.alloc("wv", [128, 512], BF16)
        wA_grp = [(0, 512), (512, 1024), (1024, 1536), (1536, 1984)]
        wA_b = [Buf("wA_g%d" % i) for i in range(4)]
        for gi in (3, 0, 1, 2):
            a, b = wA_grp[gi]
            S.dma("pool", "wA_g%d" % gi, wA.h[:, :, a:b], w_inA[l, :, a:b].rearrange("(k p) c -> p k c", p=128),
                  reads=[], writes=[wA_b[gi]])

        def wAb(col0):
            for gi, (a, b) in enumerate(wA_grp):
                if a <= col0 < b:
                    return wA_b[gi]
        S.dma("pool", "wq", wq.h[:, :, :], w_uqp[l].rearrange("(k p) c -> p k c", p=128), reads=[], writes=[wq.b])
        S.dma("pool", "wk", wk.h[:, :], w_uk[l], reads=[], writes=[wk.b])
        S.dma("pool", "wv", wv.h[:, :], w_uv[l], reads=[], writes=[wv.b])
        hTs = Ring([WORK.alloc("p1h%d" % i, [128, 8, TT], BF16) for i in range(2)])
        tmp_ring = Ring([WORK.alloc("p1tmp%d" % i, [128, TT], F32) for i in range(3)])
        if l == 0:
            xt0 = WORK.alloc("p1xt", [128, 8, TT], F32)
            sq = WORK.alloc("p1sq", [128, 8, TT], BF16)
            rstd_t = WORK.alloc("p1rstd", [128, TT], F32)
        stg = Ring([WORK.alloc("p1stg%d" % i, [128, TT], BF16) for i in range(3)])
        vstg = Ring([WORK.alloc("p1vstg%d" % i, [128, H, 65], BF16) for i in range(2)])
        for vs in vstg.items:
            S.op("dve", lambda e: e.memset(vs.h[:, :, 64:65], 1.0), writes=[vs.b])
        cs_t = Ring([WORK.alloc("p1cs%d" % i, [128, 2, TT], F32) for i in range(1)])
        cq_f = WORK.alloc("p1cqf", [128, 3, TT], F32)
        cq_sq = WORK.alloc("p1cqsq", [128, 3, TT], BF16)
        cqn = WORK.alloc("p1cqn", [128, 3, TT], BF16)
        rs_q = WORK.alloc("p1rsq", [128, TT], F32)
        rt = tmp_ring
        import os
        P1SKIP = os.environ.get("P1SKIP", "")
        NTL = int(os.environ.get("P1NT", NT))
        hmap = {}

        def get_h(t):
            a0, a1 = t * TT, (t + 1) * TT
            hT = hTs.next()
            hmap[t] = hT
            if l == 0:
                load("p1xt", xt0.h[:, :, :], xT_in[:, a0:a1].rearrange("(k p) t -> p k t", p=128), [xt0.b])
                norm_tile(xt0, lambda k: hT.h[:, k, :], hT.b, gsA, l * 8, l * 48 + 0,
                          sq.h[:, :, :], lambda k: sq.h[:, k, :], sq.b, rstd_t, tmp_ring)
                store("st_h", hT_d[:, a0:a1].rearrange("(k p) t -> p k t", p=128), hT.h[:, :, :], [hT.b])
            else:
                load(hT.b.name, hT.h[:, :, :], hT_d[:, a0:a1].rearrange("(k p) t -> p k t", p=128), [hT.b])

        get_h(0)
        for t in range(NTL):
            c0, c1 = t * TT, (t + 1) * TT
            hT = hmap.pop(t)
            cs = cs_t.next()
            load(cs.b.name + "c", cs.h[:, 0, :], cos_d[:, c0:c1], [cs.b])
            load(cs.b.name + "s", cs.h[:, 1, :], sin_d[:, c0:c1], [cs.b])

            def proj_fm(col0, m, evac):
                ps = ps_ring.next()
                for k in range(8):
                    S.op("pe", lambda e, k=k, ps=ps: e.matmul(ps.ap(0, m, 0, TT), wA.h[:, k, col0:col0 + m], hT.h[:, k, :],
                                                              start=(k == 0), stop=(k == 7)),
                         reads=[wAb(col0), hT.b], writes=ps.bufs)
                evac(ps)

            if "all" in P1SKIP:
                continue
            for j in range(3):
                def ev(ps, j=j):
                    S.op("dve", lambda e, ps=ps: e.tensor_copy(cq_f.h[:, j, :], ps.ap(0, 128, 0, TT)),
                         reads=ps.bufs, writes=[cq_f.b])
                    S.op("act", lambda e, ps=ps: e.activation(cq_sq.h[:, j, :], cq_f.h[:, j, :], AF.Square),
                         reads=[cq_f.b], writes=[cq_sq.b])
                proj_fm(1536 + j * 128, 128, ev)
            psa = ps_ring.next()
            psb = ps_ring.next()
            for (ps, col0) in ((psa, 1920), (psb, 1952)):
                for k in range(8):
                    S.op("pe", lambda e, k=k, ps=ps, col0=col0: e.matmul(
                        ps.ap(0, 32, 0, TT), wA.h[:, k, col0:col0 + 32], hT.h[:, k, :], start=(k == 0), stop=(k == 7)),
                        reads=[wAb(col0), hT.b], writes=ps.bufs)
            ta, tb = rt.next(), rt.next()
            S.op("dve", lambda e: e.tensor_tensor(ta.h[0:32, :], psa.ap(0, 32, 0, TT), cs.h[0:32, 0, :], ALU.mult),
                 reads=psa.bufs + [cs.b], writes=[ta.b])
            S.op("dve", lambda e: e.tensor_tensor(tb.h[0:32, :], psb.ap(0, 32, 0, TT), cs.h[0:32, 1, :], ALU.mult),
                 reads=psb.bufs + [cs.b], writes=[tb.b])
            st = stg.next()
            S.op("dve", lambda e, st=st: e.tensor_tensor(st.h[0:32, :], ta.h[0:32, :], tb.h[0:32, :], ALU.add),
                 reads=[ta.b, tb.b], writes=[st.b])
            store("st_" + st.b.name, kr_d[:, c0:c1], st.h[0:32, :], [st.b])
            for cch in range(0 if "qk" not in P1SKIP else 8, 8):
                def ev(ps, cch=cch):
                    st = stg.next()
                    S.op("act", lambda e, ps=ps, st=st: e.activation(st.h[:, :], ps.ap(0, 128, 0, TT), AF.Copy),
                         reads=ps.bufs, writes=[st.b])
                    dst = qna_d if cch < 4 else kna_d
                    r0 = (cch % 4) * 128
                    store("st_" + st.b.name, dst[r0:r0 + 128, c0:c1], st.h[:, :], [st.b])
                proj_fm(cch * 128, 128, ev)
            rstd_from_sq([cq_sq.h[:, 0, :], cq_sq.h[:, 1, :]], [cq_sq.b], 256, rs_q)
            rs_kv = tmp_ring.next()
            rstd_from_sq([cq_sq.h[:, 2, :]], [cq_sq.b], 128, rs_kv)
            for j in range(3):
                gcol = gq.h[:, l * 2 + j:l * 2 + j + 1] if j < 2 else gkv.h[:, l:l + 1]
                rsx = rs_q if j < 2 else rs_kv
                S.op("dve", lambda e, j=j, gcol=gcol, rsx=rsx: e.scalar_tensor_tensor(
                    cqn.h[:, j, :], cq_f.h[:, j, :], gcol, rsx.h[:, :], ALU.mult, ALU.mult),
                    reads=[cq_f.b, gq.b, gkv.b, rsx.b], writes=[cqn.b])
            if t + 1 < NTL:
                get_h(t + 1)
            for sbk in range(4 if "vna" not in P1SKIP else 0):
                ps = ps_ring.next()
                for k in range(8):
                    S.op("pe", lambda e, k=k, ps=ps, sbk=sbk: e.matmul(
                        ps.ap(0, 128, 0, 512), hT.h[:, k, sbk * 128:(sbk + 1) * 128], wA.h[:, k, 1024:1536],
                        start=(k == 0), stop=(k == 7)), reads=[wAb(1024), hT.b], writes=ps.bufs)
                st = vstg.next()
                S.op("dve", lambda e, ps=ps, st=st: e.tensor_copy(st.h[:, :, 0:64], ps.ap(0, 128, 0, 512).rearrange("p (h d) -> p h d", h=H)),
                     reads=ps.bufs, writes=[st.b])
                cidx = t * 4 + sbk
                store("st_" + st.b.name, vna_d[:, cidx, :, :], st.h[:, :, :], [st.b])
            if "qn" in P1SKIP:
                continue
            for g in range(4):
                ps = ps_ring.next()
                for j in range(2):
                    S.op("pe", lambda e, j=j, ps=ps, g=g: e.matmul(
                        ps.ap(0, 128, 0, TT), wq.h[:, j, g * 128:(g + 1) * 128], cqn.h[:, j, :],
                        start=(j == 0), stop=(j == 1)), reads=[wq.b, cqn.b], writes=ps.bufs)
                st = stg.next()
                S.op("act", lambda e, ps=ps, st=st: e.activation(st.h[:, :], ps.ap(0, 128, 0, TT), AF.Copy),
                     reads=ps.bufs, writes=[st.b])
                for hh in range(2):
                    store("st_" + st.b.name, qm_d[2 * g + hh, 0:64, c0:c1], st.h[hh * 64:(hh + 1) * 64, :], [st.b])
            if "qr" in P1SKIP:
                continue
            for g in range(2):
                psa = ps_ring.next()
                psb = ps_ring.next()
                for (ps, col0) in ((psa, 512 + g * 128), (psb, 768 + g * 128)):
                    for j in range(2):
                        S.op("pe", lambda e, j=j, ps=ps, col0=col0: e.matmul(
                            ps.ap(0, 128, 0, TT), wq.h[:, j, col0:col0 + 128], cqn.h[:, j, :],
                            start=(j == 0), stop=(j == 1)), reads=[wq.b, cqn.b], writes=ps.bufs)
                ta, tb = rt.next(), rt.next()
                S.op("dve", lambda e, psa=psa, ta=ta: e.tensor_tensor(ta.h[:, :], psa.ap(0, 128, 0, TT), cs.h[:, 0, :], ALU.mult),
                     reads=psa.bufs + [cs.b], writes=[ta.b])
                S.op("dve", lambda e, psb=psb, tb=tb: e.tensor_tensor(tb.h[:, :], psb.ap(0, 128, 0, TT), cs.h[:, 1, :], ALU.mult),
                     reads=psb.bufs + [cs.b], writes=[tb.b])
                st = stg.next()
                S.op("dve", lambda e, st=st, ta=ta, tb=tb: e.tensor_tensor(st.h[:, :], ta.h[:, :], tb.h[:, :], ALU.add),
                     reads=[ta.b, tb.b], writes=[st.b])
                for hh in range(4):
                    store("st_" + st.b.name, qm_d[4 * g + hh, 64:96, c0:c1], st.h[hh * 32:(hh + 1) * 32, :], [st.b])
            if "kn" in P1SKIP:
                continue
            for g in range(4):
                ps = ps_ring.next()
                S.op("pe", lambda e, ps=ps, g=g: e.matmul(ps.ap(0, 128, 0, TT), wk.h[:, g * 128:(g + 1) * 128], cqn.h[:, 2, :],
                                                         start=True, stop=True), reads=[wk.b, cqn.b], writes=ps.bufs)
                st = stg.next()
                S.op("act", lambda e, ps=ps, st=st: e.activation(st.h[:, :], ps.ap(0, 128, 0, TT), AF.Copy),
                     reads=ps.bufs, writes=[st.b])
                for hh in range(2):
                    store("st_" + st.b.name, km_d[2 * g + hh, :, c0:c1], st.h[hh * 64:(hh + 1) * 64, :], [st.b])
            for sbk in range(4):
                ps = ps_ring.next()
                S.op("pe", lambda e, ps=ps, sbk=sbk: e.matmul(ps.ap(0, 128, 0, 512), cqn.h[:, 2, sbk * 128:(sbk + 1) * 128],
                                                             wv.h[:, :], start=True, stop=True),
                     reads=[wv.b, cqn.b], writes=ps.bufs)
                st = vstg.next()
                S.op("dve", lambda e, ps=ps, st=st: e.tensor_copy(st.h[:, :, 0:64], ps.ap(0, 128, 0, 512).rearrange("p (h d) -> p h d", h=H)),
                     reads=ps.bufs, writes=[st.b])
                cidx = t * 4 + sbk
                store("st_" + st.b.name, vm_d[:, cidx, :, :], st.h[:, :, :], [st.b])
        barrier()

    def load_w3(l):
        A = Arena(142 * 1024, SB_LIMIT)
        wg = A.alloc("wg", [128, 8, 2048], BF16)
        wbn = A.alloc("wbn", [128, 4, 1024], BF16)
        wbm = A.alloc("wbm", [128, 4, 1024], BF16)
        wo = A.alloc("wo", [128, 8, 1024], BF16)
        for k in range(8):
            S.dma("pool", "wg", wg.h[:, k, :], w_gate[l, k * 128:(k + 1) * 128, :], reads=[], writes=[wg.b])
        S.dma("pool", "wbn", wbn.h[:, :, :], w_brna[l].rearrange("(k p) c -> p k c", p=128), reads=[], writes=[wbn.b])
        S.dma("pool", "wbm", wbm.h[:, :, :], w_brml[l].rearrange("(k p) c -> p k c", p=128), reads=[], writes=[wbm.b])
        for k in range(0, 8, 4):
            S.dma("pool", "wo", wo.h[:, k:k + 4, :], w_out[l, k * 128:(k + 4) * 128, :].rearrange("(k p) c -> p k c", p=128),
                  reads=[], writes=[wo.b])
        return wg, wbn, wbm, wo

    def load_w1ff(l):
        A = Arena(78 * 1024, 142 * 1024)
        w1 = A.alloc("w1", [128, 8, 4096], BF16)
        for k in range(8):
            S.dma("pool", "w1", w1.h[:, k, :], w_ff1[l, k * 128:(k + 1) * 128, :], reads=[], writes=[w1.b])
        return w1

    def phase2(l):
        WREG.reset()
        WORK.reset()
        A2 = Arena(WREG.base, 142 * 1024)
        na_slots = []
        ml_slots = []
        for i in range(2):
            na_slots.append(dict(
                q=A2.alloc("naq%d" % i, [128, S_TOK], BF16), k=A2.alloc("nak%d" % i, [128, S_TOK], BF16),
                v=A2.alloc("nav%d" % i, [128, 32, 65], BF16), b=A2.alloc("nab%d" % i, [128, NAB_W], BF16)))
            ml_slots.append(dict(
                q=A2.alloc("mlq%d" % i, [128, S_TOK], BF16), k=A2.alloc("mlk%d" % i, [128, S_TOK], BF16),
                v=A2.alloc("mlv%d" % i, [128, 32, 65], BF16)))
        Tt = Ring([A2.alloc("naT%d" % i, [128, 640], F32) for i in range(3)])
        Pn = Ring([A2.alloc("naP%d" % i, [128, 640], BF16) for i in range(3)])
        Pm = Ring([A2.alloc("mlP%d" % i, [128, 1024], BF16) for i in range(4)])
        recs = Ring([A2.alloc("rec%d" % i, [128, 512], F32) for i in range(3)])
        bcs = Ring([A2.alloc("bc%d" % i, [128, 512], F32) for i in range(3)])
        osts = Ring([A2.alloc("ost%d" % i, [128, 512], BF16) for i in range(2)])
        psS_n = Ring([PS(0, 2), PS(2, 2), PS(4, 2)])
        psO_r = Ring([PS(6), PS(7)])
        psB_r = psS_n

        def loads(h):
            n = na_slots[h % 2]
            m = ml_slots[h % 2]
            load(n["q"].b.name, n["q"].h[0:64, :], qna_d[h * 64:(h + 1) * 64, :], [n["q"].b])
            load(n["k"].b.name, n["k"].h[0:64, :], kna_d[h * 64:(h + 1) * 64, :], [n["k"].b])
            load(n["v"].b.name, n["v"].h[:, :, :], vna_d[:, :, h, :], [n["v"].b])
            S.dma("pool", n["b"].b.name, n["b"].h[:, :], nab_in[l, h], reads=[], writes=[n["b"].b])
            load(m["q"].b.name, m["q"].h[0:96, :], qm_d[h], [m["q"].b])
            load(m["k"].b.name + "a", m["k"].h[0:64, :], km_d[h], [m["k"].b])
            load(m["k"].b.name + "b", m["k"].h[64:96, :], kr_d, [m["k"].b])
            load(m["v"].b.name, m["v"].h[:, :, :], vm_d[:, :, h, :], [m["v"].b])

        pending = []
        recd_state = {"i": 0}
        recd_b = [Buf("recd%d" % i) for i in range(4)]

        def normalize(psO, branch, h, q0):
            rec = recs.next()
            if branch == 0:
                S.op("act", lambda e: e.activation(rec.h[64:65, :], psO.ap(64, 65, 0, 512), AF.Ln), reads=psO.bufs, writes=[rec.b])
                S.op("act", lambda e: e.activation(rec.h[64:65, :], rec.h[64:65, :], AF.Exp, scale=-1.0), reads=[rec.b], writes=[rec.b])
            else:
                S.op("dve", lambda e: e.reciprocal(rec.h[64:65, :], psO.ap(64, 65, 0, 512)), reads=psO.bufs, writes=[rec.b])

            if branch == 1:
                slot = recd_state["i"] % 4
                recd_state["i"] += 1
                bc = bcs.next()
                S.dma("sp", "recd%d" % slot, recd_d[slot:slot + 1, :], rec.h[64:65, :], reads=[rec.b], writes=[recd_b[slot]])
                S.dma("sp", bc.b.name, bc.h[0:64, :], recd_d[slot:slot + 1, :].partition_broadcast(64),
                      reads=[recd_b[slot]], writes=[bc.b])

                def part_b():
                    ost = osts.next()
                    S.op("dve", lambda e: e.tensor_tensor(ost.h[0:64, :], psO.ap(0, 64, 0, 512), bc.h[0:64, :], ALU.mult),
                         reads=psO.bufs + [bc.b], writes=[ost.b])
                    store("st_" + ost.b.name, oT_d[branch, h * 64:(h + 1) * 64, q0:q0 + 512], ost.h[0:64, :], [ost.b])
            else:
                def part_b():
                    psB = psB_r.next()
                    bc = bcs.next()
                    ost = osts.next()
                    S.op("pe", lambda e: e.matmul(psB.ap(0, 64, 0, 512), ones_f.h[64:65, 0:64], rec.h[64:65, :],
                                                  start=True, stop=True), reads=[ones_f.b, rec.b], writes=psB.bufs)
                    S.op("act", lambda e: e.activation(bc.h[0:64, :], psB.ap(0, 64, 0, 512), AF.Copy),
                         reads=psB.bufs, writes=[bc.b])
                    S.op("dve", lambda e: e.tensor_tensor(ost.h[0:64, :], psO.ap(0, 64, 0, 512), bc.h[0:64, :], ALU.mult),
                         reads=psO.bufs + [bc.b], writes=[ost.b])
                    store("st_" + ost.b.name, oT_d[branch, h * 64:(h + 1) * 64, q0:q0 + 512], ost.h[0:64, :], [ost.b])
            pending.append(part_b)

        def flush():
            while pending:
                pending.pop(0)()

        def na_head(h):
            n = na_slots[h % 2]
            q, k, v, bt = n["q"], n["k"], n["v"], n["b"]
            state = {}

            def s_mm(i):
                J = na_chunks(i)
                psS = psS_n.next()
                for s in range(5):
                    S.op("pe", lambda e, s=s: e.matmul(psS.ap(0, 128, s * 128, (s + 1) * 128),
                                                       k.h[0:64, J[s] * 128:(J[s] + 1) * 128],
                                                       q.h[0:64, i * 128:(i + 1) * 128], start=True, stop=True),
                         reads=[k.b, q.b], writes=psS.bufs)
                tt = Tt.next()
                pn = Pn.next()
                pat = na_pattern_index(i)
                S.op("dve", lambda e: e.scalar_tensor_tensor(tt.h[:, :], psS.ap(0, 128, 0, 640), float(NA_SCALE),
                                                             bt.h[:, pat * 640:(pat + 1) * 640], ALU.mult, ALU.add),
                     reads=psS.bufs + [bt.b], writes=[tt.b])
                S.op("act", lambda e: e.activation(pn.h[:, :], tt.h[:, :], AF.Exp), reads=[tt.b], writes=[pn.b])
                state[i] = (J, pn)

            def o_mm(i, psO):
                J, pn = state.pop(i)
                ib = i % 4
                for s in range(5):
                    S.op("pe", lambda e, s=s: e.matmul(psO.ap(0, 65, ib * 128, (ib + 1) * 128),
                                                       v.h[:, J[s], 0:65], pn.h[:, s * 128:(s + 1) * 128],
                                                       start=(s == 0), stop=(s == 4)),
                         reads=[v.b, pn.b], writes=psO.bufs)

            s_mm(0)
            s_mm(1)
            psO = None
            for i in range(32):
                if i % 4 == 0:
                    psO = psO_r.next()
                if i + 2 < 32:
                    s_mm(i + 2)
                o_mm(i, psO)
                if i % 4 == 1:
                    flush()
                if i % 4 == 3:
                    normalize(psO, 0, h, (i // 4) * 512)

        def mla_head(h):
            m = ml_slots[h % 2]
            q, k, v = m["q"], m["k"], m["v"]
            items = [(qt, p) for qt in range(8) for p in range(16)]
            state = {}
            psO_of = {}

            def s_mm(idx):
                qt, p = items[idx]
                psS = psS_n.next()
                for j in range(2):
                    kc = 2 * p + j
                    S.op("pe", lambda e: e.matmul(psS.ap(0, 128, j * 512, (j + 1) * 512), k.h[0:96, kc * 128:(kc + 1) * 128],
                                                  q.h[0:96, qt * 512:(qt + 1) * 512], start=True, stop=True),
                         reads=[k.b, q.b], writes=psS.bufs)
                pm = Pm.next()
                S.op("act", lambda e: e.activation(pm.h[:, :], psS.ap(0, 128, 0, 1024), AF.Exp, scale=float(MLA_SCALE)),
                     reads=psS.bufs, writes=[pm.b])
                state[idx] = pm

            s_mm(0)
            s_mm(1)
            for idx in range(len(items)):
                qt, p = items[idx]
                if p == 0:
                    psO_of[qt] = psO_r.next()
                psO = psO_of[qt]
                if idx + 2 < len(items):
                    s_mm(idx + 2)
                pm = state.pop(idx)
                for j in range(2):
                    kc = 2 * p + j
                    S.op("pe", lambda e: e.matmul(psO.ap(0, 65, 0, 512), v.h[:, kc, 0:65], pm.h[:, j * 512:(j + 1) * 512],
                                                  start=(kc == 0), stop=(kc == 31)),
                         reads=[v.b, pm.b], writes=psO.bufs)
                if p == 10:
                    flush()
                if p == 15:
                    normalize(psO, 1, h, qt * 512)

        loads(0)
        loads(1)
        w3 = load_w3(l)
        for h in range(H):
            if 1 <= h and h + 1 < H:
                loads(h + 1)
            na_head(h)
            mla_head(h)
        flush()
        barrier()
        return w3

    def phase3(l, w3):
        wg, wbn, wbm, wo = w3
        x_src = xT_in if l == 0 else xT_d
        w1 = None
        WK = Arena(6 * 1024, 78 * 1024)
        xt = WK.alloc("p3xt", [128, 8, TT], F32)
        hTs = Ring([WK.alloc("p3h%d" % i, [128, 8, TT], BF16) for i in range(2)])
        onas = Ring([WK.alloc("p3ona%d" % i, [128, 4, TT], BF16) for i in range(2)])
        omls = Ring([WK.alloc("p3oml%d" % i, [128, 4, TT], BF16) for i in range(2)])
        mg = WK.alloc("p3mg", [128, 8, TT], BF16)
        gring = Ring([WK.alloc("p3g%d" % i, [128, TT], F32) for i in range(4)])
        tring = Ring([WK.alloc("p3t%d" % i, [128, TT], F32) for i in range(4)])
        gate_col = l * 48 + 16
        cur = {}

        def loads(t):
            a0, a1 = t * TT, (t + 1) * TT
            hT, ona, oml = hTs.next(), onas.next(), omls.next()
            load(hT.b.name, hT.h[:, :, :], hT_d[:, a0:a1].rearrange("(k p) t -> p k t", p=128), [hT.b])
            load(ona.b.name, ona.h[:, :, :], oT_d[0, :, a0:a1].rearrange("(k p) t -> p k t", p=128), [ona.b])
            load(oml.b.name, oml.h[:, :, :], oT_d[1, :, a0:a1].rearrange("(k p) t -> p k t", p=128), [oml.b])
            cur[t] = (hT, ona, oml)

        loads(0)
        for t in range(NT):
            c0, c1 = t * TT, (t + 1) * TT
            hT, ona, oml = cur.pop(t)
            load("p3xt", xt.h[:, :, :], x_src[:, c0:c1].rearrange("(k p) t -> p k t", p=128), [xt.b])
            if t + 1 < NT:
                loads(t + 1)
            if t == 1:
                w1 = load_w1ff(l)
            for c in range(8):
                gts = []
                for br in range(2):
                    ps = ps_ring7.next()
                    colw = br * 1024 + c * 128
                    for k in range(8):
                        S.op("pe", lambda e: e.matmul(ps.ap(0, 128, 0, TT), wg.h[:, k, colw:colw + 128], hT.h[:, k, :],
                                                      start=(k == 0), stop=(k == 7)), reads=[wg.b, hT.b], writes=ps.bufs)
                    g = gring.next()
                    bcol = l * 16 + br * 8 + c
                    S.op("act", lambda e: e.activation(g.h[:, :], ps.ap(0, 128, 0, TT), AF.Sigmoid,
                                                       bias=bgate.h[:, bcol:bcol + 1]), reads=ps.bufs + [bgate.b], writes=[g.b])
                    gts.append(g)
                tts = []
                for br, (w, o) in enumerate(((wbn, ona), (wbm, oml))):
                    ps = ps_ring7.next()
                    for k in range(4):
                        S.op("pe", lambda e: e.matmul(ps.ap(0, 128, 0, TT), w.h[:, k, c * 128:(c + 1) * 128], o.h[:, k, :],
                                                      start=(k == 0), stop=(k == 3)), reads=[w.b, o.b], writes=ps.bufs)
                    tt = tring.next()
                    S.op("dve", lambda e: e.tensor_tensor(tt.h[:, :], ps.ap(0, 128, 0, TT), gts[br].h[:, :], ALU.mult),
                         reads=ps.bufs + [gts[br].b], writes=[tt.b])
                    tts.append(tt)
                S.op("dve", lambda e: e.tensor_tensor(mg.h[:, c, :], tts[0].h[:, :], tts[1].h[:, :], ALU.add),
                     reads=[tts[0].b, tts[1].b], writes=[mg.b])
            def mm_out(c2, ps):
                for k in range(8):
                    S.op("pe", lambda e: e.matmul(ps.ap(0, 128, 0, TT), wo.h[:, k, c2 * 128:(c2 + 1) * 128], mg.h[:, k, :],
                                                  start=(k == 0), stop=(k == 7)), reads=[wo.b, mg.b], writes=ps.bufs)
            resid_norm_loop(xt, hT, gate_col, mm_out, ps_ring7)
            store("st_p3xt", xT_d[:, c0:c1].rearrange("(k p) t -> p k t", p=128), xt.h[:, :, :], [xt.b])
            rstd_t = gring.next()
            rstd_finish(ps_ss, D, rstd_t)
            for k in range(8):
                tmp = tring.next()
                S.op("dve", lambda e: e.tensor_tensor(tmp.h[:, :], xt.h[:, k, :], rstd_t.h[:, :], ALU.mult),
                     reads=[xt.b, rstd_t.b], writes=[tmp.b])
                S.op("act", lambda e: e.activation(hT.h[:, k, :], tmp.h[:, :], AF.Identity,
                                                   bias=modv.h[:, l * 48 + 24 + k:l * 48 + 25 + k],
                                                   scale=gsM.h[:, l * 8 + k:l * 8 + k + 1]),
                     reads=[tmp.b, modv.b, gsM.b], writes=[hT.b])
            store("st_" + hT.b.name, hT_d[:, c0:c1].rearrange("(k p) t -> p k t", p=128), hT.h[:, :, :], [hT.b])
        barrier()
        return w1

    def phase4(l, last, w1):
        A = Arena(142 * 1024, SB_LIMIT)
        w2 = A.alloc("w2", [128, 32, 1024], BF16)
        for f in range(0, 32, 4):
            S.dma("pool", "w2", w2.h[:, f:f + 4, :], w_ff2[l, f * 128:(f + 4) * 128, :].rearrange("(f p) c -> p f c", p=128),
                  reads=[], writes=[w2.b])
        WK = Arena(6 * 1024, 78 * 1024)
        xt = WK.alloc("p4xt", [128, 8, TT], F32)
        h2s = Ring([WK.alloc("p4h%d" % i, [128, 8, TT], BF16) for i in range(2)])
        aT = WK.alloc("p4a", [128, 32, TT], BF16)
        rstd_t = WK.alloc("p4rstd", [128, TT], F32)
        tmp_ring = Ring([WK.alloc("p4tmp%d" % i, [128, TT], F32) for i in range(3)])
        gate_col = l * 48 + 40
        cur = {}

        def loads(t):
            a0, a1 = t * TT, (t + 1) * TT
            h2 = h2s.next()
            load(h2.b.name, h2.h[:, :, :], hT_d[:, a0:a1].rearrange("(k p) t -> p k t", p=128), [h2.b])
            cur[t] = h2

        loads(0)
        for t in range(NT):
            c0, c1 = t * TT, (t + 1) * TT
            h2 = cur.pop(t)
            load("p4xt", xt.h[:, :, :], xT_d[:, c0:c1].rearrange("(k p) t -> p k t", p=128), [xt.b])
            if t + 1 < NT:
                loads(t + 1)
            for f in range(32):
                ps = ps_ring7.next()
                for k in range(8):
                    S.op("pe", lambda e: e.matmul(ps.ap(0, 128, 0, TT), w1.h[:, k, f * 128:(f + 1) * 128], h2.h[:, k, :],
                                                  start=(k == 0), stop=(k == 7)), reads=[w1.b, h2.b], writes=ps.bufs)
                tmp = tmp_ring.next()
                S.op("act", lambda e: e.activation(tmp.h[:, :], ps.ap(0, 128, 0, TT), AF.Relu), reads=ps.bufs, writes=[tmp.b])
                S.op("dve", lambda e: e.tensor_tensor(aT.h[:, f, :], tmp.h[:, :], tmp.h[:, :], ALU.mult),
                     reads=[tmp.b], writes=[aT.b])
            def mm_ff2(c2, ps):
                for f in range(32):
                    S.op("pe", lambda e: e.matmul(ps.ap(0, 128, 0, TT), w2.h[:, f, c2 * 128:(c2 + 1) * 128], aT.h[:, f, :],
                                                  start=(f == 0), stop=(f == 31)), reads=[w2.b, aT.b], writes=ps.bufs)
            resid_norm_loop(xt, h2, gate_col, mm_ff2, ps_ring7)
            rstd_finish(ps_ss, D, rstd_t)
            if not last:
                store("st_p4xt", xT_d[:, c0:c1].rearrange("(k p) t -> p k t", p=128), xt.h[:, :, :], [xt.b])
                for k in range(8):
                    tmp = tmp_ring.next()
                    S.op("dve", lambda e: e.tensor_tensor(tmp.h[:, :], xt.h[:, k, :], rstd_t.h[:, :], ALU.mult),
                         reads=[xt.b, rstd_t.b], writes=[tmp.b])
                    S.op("act", lambda e: e.activation(h2.h[:, k, :], tmp.h[:, :], AF.Identity,
                                                       bias=modv.h[:, (l + 1) * 48 + k:(l + 1) * 48 + k + 1],
                                                       scale=gsA.h[:, (l + 1) * 8 + k:(l + 1) * 8 + k + 1]),
                         reads=[tmp.b, modv.b, gsA.b], writes=[h2.b])
                store("st_" + h2.b.name, hT_d[:, c0:c1].rearrange("(k p) t -> p k t", p=128), h2.h[:, :, :], [h2.b])
            else:
                for k in range(8):
                    tmp = tmp_ring.next()
                    S.op("dve", lambda e: e.tensor_tensor(tmp.h[:, :], xt.h[:, k, :], rstd_t.h[:, :], ALU.mult),
                         reads=[xt.b, rstd_t.b], writes=[tmp.b])
                    S.op("act", lambda e: e.activation(xt.h[:, k, :], tmp.h[:, :], AF.Copy, scale=gfin.h[:, k:k + 1]),
                         reads=[tmp.b, gfin.b], writes=[xt.b])
                store("st_out", outT[:, c0:c1].rearrange("(k p) t -> p k t", p=128), xt.h[:, :, :], [xt.b])
        barrier()

    phase0()
    for l in range(Ln):
        phase1(l)
        if stop_after == (l, "p1"):
            break
        w3 = phase2(l)
        if stop_after == (l, "p2"):
            break
        w1 = phase3(l, w3)
        if stop_after == (l, "p3"):
            break
        phase4(l, l == Ln - 1, w1)
    barrier()
    S.emit()
    return nc, S


_CACHE = {}


def _colsT(v, n):
    return np.ascontiguousarray(np.swapaxes(v.reshape(v.shape[:-1] + (n, 128)), -1, -2))


def _colsTL(v, n):
    t = _colsT(v, n)
    return np.ascontiguousarray(t.transpose(1, 0, 2).reshape(128, -1))


def prep_shared(inp):
    L = L_DEPTH
    w_in = inp["w_in"]
    kr = w_in[:, :, 1920:1952]
    kr_sw = np.concatenate([kr[:, :, 16:32], kr[:, :, 0:16]], axis=-1)
    w_inA = np.ascontiguousarray(np.concatenate([w_in[:, :, :1952], kr_sw], axis=-1))
    w_gate = np.ascontiguousarray(w_in[:, :, 1952:])
    wuq = inp["w_uq"].reshape(L, 256, H, 96)
    nope = wuq[:, :, :, :64].reshape(L, 256, 512)
    rope = wuq[:, :, :, 64:]
    rope_sw = np.concatenate([rope[..., 16:32], rope[..., 0:16]], axis=-1)
    w_uqp = np.ascontiguousarray(np.concatenate([nope, rope.reshape(L, 256, 256), rope_sw.reshape(L, 256, 256)], axis=-1))
    wukv = inp["w_ukv"].reshape(L, 128, H, 128)
    w_uk = np.ascontiguousarray(wukv[:, :, :, :64].reshape(L, 128, 512))
    w_uv = np.ascontiguousarray(wukv[:, :, :, 64:].reshape(L, 128, 512))
    idx_r, idx_c, msk = build_na_index()
    rpb = inp["rpb"]
    nab = np.where(msk[None, None], rpb[:, :, idx_r, idx_c], np.float32(MASK_VAL)).astype(np.float32)
    nab = np.ascontiguousarray(nab.transpose(0, 1, 3, 2, 4).reshape(L, H, 128, NAB_W))
    p = np.arange(128)
    i32 = p % 32
    invf = (1.0 / (10000.0 ** (np.arange(0, 32, 2, dtype=np.float32) / 32.0))).astype(np.float32)
    cst = np.zeros((128, 4), np.float32)
    cst[:, 0] = invf[i32 % 16]
    cst[:, 1] = np.where(i32 < 16, -1.0, 1.0)
    return dict(
        cst=cst, w_ada=inp["w_ada"], b_adaT=_colsTL(inp["b_ada"], 48), g_mixT=_colsTL(inp["g_mix"], 8),
        g_mlpT=_colsTL(inp["g_mlp"], 8), g_finT=_colsT(inp["g_final"], 8), w_inA=w_inA, w_gate=w_gate,
        b_gateT=_colsTL(inp["b_gate"], 16), g_qT=_colsTL(inp["g_q"], 2), g_kvT=_colsTL(inp["g_kv"], 1),
        w_uqp=w_uqp, w_uk=w_uk, w_uv=w_uv, w_br_na=inp["w_br_na"], w_br_mla=inp["w_br_mla"], w_out=inp["w_out"],
        w_ff1=inp["w_ff1"], w_ff2=inp["w_ff2"], nab=nab)


def make_in_maps(inp, cores):
    inp = {k: np.asarray(v) for k, v in inp.items()}
    shared = prep_shared(inp)
    maps = []
    for b in cores:
        m = dict(shared)
        m["xT"] = np.ascontiguousarray(inp["x"][b].T)
        m["cT"] = _colsT(inp["c"][b], 8)
        m["pos"] = np.ascontiguousarray(inp["positions"][b].reshape(1, S_TOK).astype(np.int32))
        maps.append(m)
    return maps


def kernel(**inputs):
    if "nc" not in _CACHE:
        _CACHE["nc"] = build_program()[0]
    nc = _CACHE["nc"]
    in_maps = make_in_maps(inputs, list(range(8)))
    res = run_bass_kernel_spmd(nc, in_maps, core_ids=list(range(8)))
    out = np.stack([np.ascontiguousarray(r["outT"].T) for r in res.results], axis=0)
    return out.astype(np.float32)
```
